# Optimizing a Trainium2 kernel written in Bass

```python
import math
import jax, jax.numpy as jnp
from jax import lax
import numpy as np

D_MODEL = 2048
BATCH = 2
SEQ = 4096
DEPTH = 2

CHUNK = 64
Q_BLOCK = 128
NORM_EPS = 1e-6

A_HEADS = 8
A_HEAD_DIM = 128
A_WIDTH = A_HEADS * 2 * A_HEAD_DIM
B_GROUPS = 8
B_WIDTH = D_MODEL
B_GROUP_DIM = B_WIDTH // B_GROUPS
B_SPAN = 128
C_WIDTH = (4 * D_MODEL // 3) // 128 * 128
C_HEADS = 16
C_BLOCK = C_WIDTH // C_HEADS
C_CONV = 4
C_GATE_C = 8.0

N_EVEN = (DEPTH + 1) // 2
N_ODD = DEPTH // 2
AB_IN = 4 * A_WIDTH + 3 * B_WIDTH
AB_MIX = A_WIDTH + B_WIDTH

kernel_name = "chunk_causal_diffattn_gmlp_rglru_hybrid"


def rms_norm(x, g):
    xf = x.astype(jnp.float32)
    y = xf * lax.rsqrt(jnp.mean(xf * xf, axis=-1, keepdims=True) + NORM_EPS)
    return (y * g.astype(jnp.float32)).astype(x.dtype)


def layer_norm(x, g, b):
    xf = x.astype(jnp.float32)
    mu = jnp.mean(xf, axis=-1, keepdims=True)
    xc = xf - mu
    y = xc * lax.rsqrt(jnp.mean(xc * xc, axis=-1, keepdims=True) + NORM_EPS)
    return (y * g.astype(jnp.float32) + b.astype(jnp.float32)).astype(x.dtype)


def diff_attention(q, k, v, lam):
    bsz, seq = q.shape[0], q.shape[1]
    nb = seq // Q_BLOCK
    k1, k2 = k[..., 0, :], k[..., 1, :]
    key_chunk = jnp.arange(seq) // CHUNK
    qb = q.reshape(bsz, nb, Q_BLOCK, A_HEADS, 2, A_HEAD_DIM).transpose(1, 0, 2, 3, 4, 5)

    def block(args):
        qblk, bi = args
        q_chunk = (bi * Q_BLOCK + jnp.arange(Q_BLOCK)) // CHUNK
        mask = key_chunk[None, :] <= q_chunk[:, None]
        s1 = jnp.einsum('bqhd,bkhd->bhqk', qblk[..., 0, :], k1).astype(jnp.float32)
        s2 = jnp.einsum('bqhd,bkhd->bhqk', qblk[..., 1, :], k2).astype(jnp.float32)
        p1 = jax.nn.softmax(jnp.where(mask, s1, -jnp.inf), axis=-1)
        p2 = jax.nn.softmax(jnp.where(mask, s2, -jnp.inf), axis=-1)
        w = (p1 - lam * p2).astype(v.dtype)
        return jnp.einsum('bhqk,bkhe->bqhe', w, v)

    out = lax.map(block, (qb, jnp.arange(nb)))
    return out.transpose(1, 0, 2, 3, 4).reshape(bsz, seq, A_HEADS, 2 * A_HEAD_DIM)


def spatial_gating(u, v, ln_g, ln_b, w_s, b_s):
    bsz, seq = v.shape[0], v.shape[1]
    n = seq // B_SPAN
    v = layer_norm(v, ln_g, ln_b)
    vg = v.reshape(bsz, n, B_SPAN, B_GROUPS, B_GROUP_DIM)
    pos_chunk = jnp.arange(B_SPAN) // CHUNK
    mask = pos_chunk[:, None] >= pos_chunk[None, :]
    w = jnp.where(mask[None], w_s, jnp.zeros_like(w_s))
    mixed = jnp.einsum('gij,bnjgc->bnigc', w, vg) + b_s.T[None, None, :, :, None]
    return u * mixed.reshape(bsz, seq, B_WIDTH)


def causal_depthwise_conv(x, w, b):
    seq = x.shape[1]
    xp = jnp.pad(x, ((0, 0), (C_CONV - 1, 0), (0, 0)))
    y = xp[:, 0:seq] * w[0]
    for t in range(1, C_CONV):
        y = y + xp[:, t:t + seq] * w[t]
    return y + b


def rg_lru(x, w_a, b_a, w_x, b_x, lam):
    bsz, seq = x.shape[0], x.shape[1]
    xh = x.reshape(bsz, seq, C_HEADS, C_BLOCK)
    r = jax.nn.sigmoid((jnp.einsum('bshi,hij->bshj', xh, w_a).reshape(bsz, seq, C_WIDTH) + b_a).astype(jnp.float32))
    i = jax.nn.sigmoid((jnp.einsum('bshi,hij->bshj', xh, w_x).reshape(bsz, seq, C_WIDTH) + b_x).astype(jnp.float32))
    log_a = -C_GATE_C * r * jax.nn.softplus(-lam.astype(jnp.float32))
    a = jnp.exp(log_a)
    beta = jnp.sqrt(-jnp.expm1(2.0 * log_a))
    bt = beta * (i * x.astype(jnp.float32))

    def combine(lhs, rhs):
        a_l, b_l = lhs
        a_r, b_r = rhs
        return a_l * a_r, a_r * b_l + b_r

    _, h = lax.associative_scan(combine, (a, bt), axis=1)
    return h.astype(x.dtype)


def setup_inputs(seed: int = 0) -> dict:
    key = jax.random.key(seed)
    ks = jax.random.split(key, 24)
    f32 = jnp.float32
    nrm = lambda k, shape, s: (jax.random.normal(k, shape, f32) * s)
    x = jax.random.normal(ks[0], (BATCH, SEQ, D_MODEL), f32)
    ab_norm = 1.0 + nrm(ks[1], (N_EVEN, D_MODEL), 0.01)
    ab_w_in = nrm(ks[2], (N_EVEN, D_MODEL, AB_IN), D_MODEL ** -0.5)
    ab_lambda = nrm(ks[3], (N_EVEN, 4, A_HEAD_DIM), 0.1)
    ab_head_norm = 1.0 + nrm(ks[4], (N_EVEN, A_WIDTH), 0.01)
    ab_sgu_ln_g = 1.0 + nrm(ks[5], (N_EVEN, B_WIDTH), 0.01)
    ab_sgu_ln_b = nrm(ks[6], (N_EVEN, B_WIDTH), 0.01)
    ab_sgu_w = nrm(ks[7], (N_EVEN, B_GROUPS, B_SPAN, B_SPAN), B_SPAN ** -0.5)
    ab_sgu_b = 1.0 + nrm(ks[8], (N_EVEN, B_GROUPS, B_SPAN), 0.01)
    ab_w_out = nrm(ks[9], (N_EVEN, AB_MIX, D_MODEL), (AB_MIX * 2 * DEPTH) ** -0.5)
    c_norm = 1.0 + nrm(ks[10], (N_ODD, D_MODEL), 0.01)
    c_w_in = nrm(ks[11], (N_ODD, D_MODEL, 2 * C_WIDTH), D_MODEL ** -0.5)
    c_conv_w = nrm(ks[12], (N_ODD, C_CONV, C_WIDTH), C_CONV ** -0.5)
    c_conv_b = nrm(ks[13], (N_ODD, C_WIDTH), 0.01)
    c_gate_a_w = nrm(ks[14], (N_ODD, C_HEADS, C_BLOCK, C_BLOCK), C_BLOCK ** -0.5)
    c_gate_a_b = nrm(ks[15], (N_ODD, C_WIDTH), 0.01)
    c_gate_x_w = nrm(ks[16], (N_ODD, C_HEADS, C_BLOCK, C_BLOCK), C_BLOCK ** -0.5)
    c_gate_x_b = nrm(ks[17], (N_ODD, C_WIDTH), 0.01)
    a_pow_c = jax.random.uniform(ks[18], (N_ODD, C_WIDTH), f32, 0.9, 0.999)
    a0 = a_pow_c ** (1.0 / C_GATE_C)
    c_lambda = jnp.log(a0) - jnp.log1p(-a0)
    c_w_out = nrm(ks[19], (N_ODD, C_WIDTH, D_MODEL), (C_WIDTH * 2 * DEPTH) ** -0.5)
    final_norm = 1.0 + nrm(ks[20], (D_MODEL,), 0.01)
    return {"x": x, "ab_norm": ab_norm, "ab_w_in": ab_w_in, "ab_lambda": ab_lambda,
            "ab_head_norm": ab_head_norm, "ab_sgu_ln_g": ab_sgu_ln_g, "ab_sgu_ln_b": ab_sgu_ln_b,
            "ab_sgu_w": ab_sgu_w, "ab_sgu_b": ab_sgu_b, "ab_w_out": ab_w_out,
            "c_norm": c_norm, "c_w_in": c_w_in, "c_conv_w": c_conv_w, "c_conv_b": c_conv_b,
            "c_gate_a_w": c_gate_a_w, "c_gate_a_b": c_gate_a_b, "c_gate_x_w": c_gate_x_w,
            "c_gate_x_b": c_gate_x_b, "c_lambda": c_lambda, "c_w_out": c_w_out,
            "final_norm": final_norm}


def reference(x, ab_norm, ab_w_in, ab_lambda, ab_head_norm, ab_sgu_ln_g, ab_sgu_ln_b,
              ab_sgu_w, ab_sgu_b, ab_w_out, c_norm, c_w_in, c_conv_w, c_conv_b,
              c_gate_a_w, c_gate_a_b, c_gate_x_w, c_gate_x_b, c_lambda, c_w_out, final_norm):
    bsz, seq = x.shape[0], x.shape[1]
    for l in range(DEPTH):
        if l % 2 == 0:
            e = l // 2
            lam_init = 0.8 - 0.6 * math.exp(-0.3 * l)
            xn = rms_norm(x, ab_norm[e])
            proj = xn @ ab_w_in[e]
            q, k, v, z_a, u_b, v_b, z_b = jnp.split(
                proj, np.cumsum([A_WIDTH, A_WIDTH, A_WIDTH, A_WIDTH, B_WIDTH, B_WIDTH]), axis=-1)
            q = q.reshape(bsz, seq, A_HEADS, 2, A_HEAD_DIM) * (A_HEAD_DIM ** -0.5)
            k = k.reshape(bsz, seq, A_HEADS, 2, A_HEAD_DIM)
            v = v.reshape(bsz, seq, A_HEADS, 2 * A_HEAD_DIM)
            lp = ab_lambda[e].astype(jnp.float32)
            lam = jnp.exp(jnp.dot(lp[0], lp[1])) - jnp.exp(jnp.dot(lp[2], lp[3])) + lam_init
            attn = diff_attention(q, k, v, lam)
            attn = rms_norm(attn, ab_head_norm[e].reshape(A_HEADS, 2 * A_HEAD_DIM)) * (1.0 - lam_init)
            y_a = attn.reshape(bsz, seq, A_WIDTH) * jax.nn.silu(z_a)
            sgu = spatial_gating(jax.nn.gelu(u_b), jax.nn.gelu(v_b), ab_sgu_ln_g[e], ab_sgu_ln_b[e],
                                 ab_sgu_w[e], ab_sgu_b[e])
            y_b = sgu * jax.nn.silu(z_b)
            y = jnp.concatenate([y_a, y_b], axis=-1) @ ab_w_out[e]
        else:
            o = l // 2
            xn = rms_norm(x, c_norm[o])
            proj = xn @ c_w_in[o]
            xb, z_c = jnp.split(proj, 2, axis=-1)
            xb = causal_depthwise_conv(xb, c_conv_w[o], c_conv_b[o])
            h = rg_lru(xb, c_gate_a_w[o], c_gate_a_b[o], c_gate_x_w[o], c_gate_x_b[o], c_lambda[o])
            y = (h * jax.nn.silu(z_c)) @ c_w_out[o]
        x = x + y
    return rms_norm(x, final_norm)
```

```python
import numpy as np
import concourse.bass as bass
import concourse.mybir as mybir
from concourse.bass_utils import run_bass_kernel_spmd

F32 = mybir.dt.float32
BF16 = mybir.dt.bfloat16
AF = mybir.ActivationFunctionType
ALU = mybir.AluOpType
AX = mybir.AxisListType

T = 1024
NT = 8
D = 2048
KC = 16
AB_IN = 14336
CW = 2688
NCH = 21
EPS = 1e-6
LAM_INIT = 0.2
NEG = -30000.0
RG = [[0, 1, 2, 3], [4, 5, 6, 7]]


class Tok:
    __slots__ = ("src", "val")

    def __init__(self, src, val):
        self.src = src
        self.val = val


class Buf:
    __slots__ = ("name", "writer", "extra", "readers")

    def __init__(self, name=""):
        self.name = name
        self.writer = None
        self.extra = {}
        self.readers = []


class DSem:
    def __init__(self, K, name, step=16):
        self.sem = K.nc.alloc_semaphore(name)
        self.count = 0
        self.is_dma = True
        self.step = step
        K.dsems.append(self)


class Eng:
    def __init__(self, K, name, eng):
        self.K = K
        self.name = name
        self.eng = eng
        self.sem = K.nc.alloc_semaphore("prog_" + name)
        self.count = 0
        self.last_ins = None
        self.last_signaled = True
        self.waited = {}
        self.is_dma = False
        self.nwaits = 0

    def _ensure_signaled(self, val):
        if val > self.count:
            assert val == self.count + 1 and self.last_ins is not None and not self.last_signaled, \
                (self.name, val, self.count)
            self.last_ins.then_inc(self.sem, 1)
            self.count += 1
            self.last_signaled = True

    def wait(self, tok, raw=True):
        if tok is None:
            return
        src = tok.src
        if src is self:
            if (not raw) or self.name in ("pe", "sp"):
                return
        if src.is_dma:
            val = src.count
        else:
            src._ensure_signaled(tok.val)
            val = tok.val
        if self.waited.get(id(src), 0) >= val:
            return
        self.eng.wait_ge(src.sem, val)
        self.nwaits += 1
        self.waited[id(src)] = val

    def deps(self, reads, writes, parts):
        for b in reads:
            self.wait(b.writer)
            for w in list(b.extra.values()):
                self.wait(w)
        for b in writes:
            self.wait(b.writer, raw=False)
            for w in list(b.extra.values()):
                self.wait(w, raw=False)
            for r in b.readers:
                self.wait(r, raw=False)
        for b in parts:
            for r in b.readers:
                self.wait(r, raw=False)

    def _record(self, tok, reads, writes, parts):
        for b in reads:
            b.readers.append(tok)
        for b in writes:
            b.writer = tok
            b.extra = {}
            b.readers = []
        for b in parts:
            b.extra[id(tok.src)] = tok

    def op(self, fn, reads=(), writes=(), parts=(), signal=False):
        self.deps(reads, writes, parts)
        ins = fn(self.eng)
        self.last_ins = ins
        self.last_signaled = False
        tok = Tok(self, self.count + 1)
        if signal or self.name != "pe":
            self._ensure_signaled(self.count + 1)
        self._record(tok, reads, writes, parts)
        return ins

    def dma(self, dsem, out, in_, reads=(), writes=(), parts=(), **kw):
        self.deps(reads, writes, parts)
        ins = self.eng.dma_start(out=out, in_=in_, **kw)
        ins.then_inc(dsem.sem, 16)
        dsem.count += 16
        self._record(Tok(dsem, dsem.count), reads, writes, parts)
        return ins

    def all_gather(self, dsem, out, in_, reads=(), writes=(), qos=None):
        self.deps(reads, writes, ())
        if qos is None:
            ins = self.eng.collective_compute("AllGather", ALU.bypass, replica_groups=RG, ins=[in_], outs=[out])
        else:
            ins = self.eng.collective_compute("AllGather", ALU.bypass, replica_groups=RG, ins=[in_], outs=[out], dma_qos=qos)
        ins.then_inc(dsem.sem, 1)
        dsem.count += 1
        self._record(Tok(dsem, dsem.count), reads, writes, ())
        return ins


class Kern:
    def __init__(self, nc):
        self.nc = nc
        self.dsems = []
        self.pe = Eng(self, "pe", nc.tensor)
        self.act = Eng(self, "act", nc.scalar)
        self.dve = Eng(self, "dve", nc.vector)
        self.pool = Eng(self, "pool", nc.gpsimd)
        self.sp = Eng(self, "sp", nc.sync)
        self.engs = [self.pe, self.act, self.dve, self.pool, self.sp]

    def dsem(self, name, step=16):
        return DSem(self, name, step)

    def barrier(self):
        sp = self.sp
        for e in self.engs:
            if e.last_ins is not None and not e.last_signaled:
                e._ensure_signaled(e.count + 1)
        for s in self.engs:
            if s is not sp and s.count > 0:
                sp.wait(Tok(s, s.count))
        for d in self.dsems:
            if d.count > 0 and d.step == 16:
                sp.wait(Tok(d, d.count))
        sp.eng.sem_inc(sp.sem, 1)
        sp.count += 1
        sp.last_signaled = True
        for e in self.engs:
            if e is not sp:
                e.wait(Tok(sp, sp.count))


def build_program(stop=99, dbg=False):
    nc = bass.Bass("TRN2", target_bir_lowering=False)
    K = Kern(nc)
    pe, act, dve, pool, sp = K.pe, K.act, K.dve, K.pool, K.sp

    def din(name, shape, dt=F32):
        return nc.dram_tensor(name, shape, dt, kind="ExternalInput").ap()

    def dscr(name, shape, dt):
        return nc.dram_tensor(name, shape, dt).ap()

    x_d = din("x", [T, D])
    w_in = din("w_in", [D, AB_IN])
    w_out = din("w_out", [4096, D])
    cw_in = din("cw_in", [D, 2 * CW])
    cw_out = din("cw_out", [CW, D])
    vecs = din("vecs", [6, D])
    lam_p = din("lam_p", [512])
    sgu_wT = din("sgu_wT", [128, 8, 128])
    sgu_b = din("sgu_b", [128, 8])
    conv_w = din("conv_w", [128, NCH, 4])
    chv = din("chv", [128, 4, NCH])
    ga_d = din("ga", [NCH, 128, 4, 128])
    gx_d = din("gx", [NCH, 128, 4, 128])
    ident_d = din("ident", [128, 128])
    masks_d = din("masks", [128, 16])
    y_d = nc.dram_tensor("y", [T, D], F32, kind="ExternalOutput").ap()

    qT_d = dscr("qT_d", [128, 16 * T], BF16)
    kvl = [dscr("kvl%d" % h, [256, 2048], BF16) for h in range(8)]
    kvg = [dscr("kvg%d" % h, [1024, 2048], BF16) for h in range(8)]
    sza_d = dscr("sza_d", [T, D], BF16)
    ymix_d = dscr("ymix_d", [T, 4096], BF16)
    x1_d = dscr("x1_d", [T, D], F32)
    halo_l = dscr("halo_l", [128, NCH * 3], BF16)
    halo_g = dscr("halo_g", [512, NCH * 3], BF16)
    st_l = dscr("st_l", [128, 2 * NCH], F32)
    st_g = dscr("st_g", [512, 2 * NCH], F32)

    dbg_out = {}

    def dbg_dump(name, src_ap, shape, dt, reads):
        o = nc.dram_tensor("dbg_" + name, shape, dt, kind="ExternalOutput").ap()
        ds = K.dsem("dbg_" + name)
        b = Buf()
        sp.dma(ds, o, src_ap, reads=reads, writes=[b])
        dbg_out[name] = b

    B_qT = [Buf() for _ in range(16)]
    B_kvl = [Buf() for _ in range(8)]
    B_kvg = [Buf() for _ in range(8)]
    B_sza = Buf()
    B_ymix = Buf()
    B_x1 = Buf()

    PS = [nc.alloc_psum_tensor("ps%d" % i, [128, 512], F32) for i in range(8)]
    PB = [Buf("ps%d" % i) for i in range(8)]

    ident = nc.alloc_sbuf_tensor("ident_sb", [128, 128], BF16)
    masks = nc.alloc_sbuf_tensor("masks_sb", [128, 16], F32)
    ones_b = nc.alloc_sbuf_tensor("ones_b", [128, 2], BF16)
    B_const = Buf("const")
    ds_c = K.dsem("const")
    pool.dma(ds_c, ident[:], ident_d, parts=[B_const])
    sp.dma(ds_c, masks[:], masks_d, parts=[B_const])
    dve.op(lambda e: e.memset(ones_b[:], 1.0), parts=[B_const])

    rr = [0]

    def evac(out, in_, reads, writes=(), scale=None, eng=None, parts=()):
        if eng is None:
            eng = act if (rr[0] % 2 == 0) else dve
            rr[0] += 1
        if eng is act:
            if scale is None:
                act.op(lambda e: e.activation(out=out, in_=in_, func=AF.Copy), reads=reads, writes=writes, parts=parts)
            else:
                act.op(lambda e: e.activation(out=out, in_=in_, func=AF.Copy, scale=float(scale)),
                       reads=reads, writes=writes, parts=parts)
        else:
            if scale is None:
                dve.op(lambda e: e.tensor_copy(out=out, in_=in_), reads=reads, writes=writes, parts=parts)
            else:
                dve.op(lambda e: e.tensor_scalar(out=out, in0=in_, scalar1=float(scale), scalar2=None,
                                                 op0=ALU.mult), reads=reads, writes=writes, parts=parts)

    wball = nc.alloc_sbuf_tensor("wball", [128, 3 * KC * 512], BF16)
    WB = [wball[:, i * KC * 512:(i + 1) * KC * 512].rearrange("p (a b) -> p a b", a=KC) for i in range(3)]
    B_wb = [Buf("wb%d" % i) for i in range(3)]
    ds_wb = [K.dsem("wb%d" % i) for i in range(3)]

    wslot_ctr = [0]

    def run_stream(jobs, nslots=3):
        wj = [j for j in jobs if j[0] is not None]
        base = wslot_ctr[0]
        wslot_ctr[0] += len(wj)
        slot_of = {id(j): (base + k) % nslots for k, j in enumerate(wj)}
        for k in range(min(nslots, len(wj))):
            wj[k][0]((base + k) % nslots)
        nxt = min(nslots, len(wj))
        for job in jobs:
            if job[0] is None:
                job[1](None)
                continue
            job[1](slot_of[id(job)])
            if nxt < len(wj):
                wj[nxt][0]((base + nxt) % nslots)
                nxt += 1

    def wload(src2d, col0, ncols):
        def f(slot):
            pool.dma(ds_wb[slot], WB[slot][:, :, 0:ncols],
                     src2d.rearrange("(kc p) n -> p kc n", p=128)[:, :, col0:col0 + ncols],
                     writes=[B_wb[slot]])
        return f

    def norm_phase(src_d, B_src, gvec_row, xnT, B_xn, tg):
        with nc.sbuf_tensor("gbc" + tg, [128, D], F32) as gbc, \
             nc.sbuf_tensor("xt0" + tg, [128, D], F32) as xt0, nc.sbuf_tensor("xt1" + tg, [128, D], F32) as xt1, \
             nc.sbuf_tensor("junk" + tg, [128, D], BF16) as junk, \
             nc.sbuf_tensor("xn0" + tg, [128, D], BF16) as xn0, nc.sbuf_tensor("xn1" + tg, [128, D], BF16) as xn1, \
             nc.sbuf_tensor("nst" + tg, [128, 4 * NT], F32) as nst:
            xts = [xt0, xt1]
            xns = [xn0, xn1]
            B_xt = [Buf(), Buf()]
            B_xnk = [Buf(), Buf()]
            B_g = Buf()
            B_junk = Buf()
            B_st = [Buf() for _ in range(NT)]
            ds_x = [K.dsem("nx0" + tg), K.dsem("nx1" + tg)]
            ds_g = K.dsem("ng" + tg)
            sp.dma(ds_g, gbc[:], vecs[gvec_row, :].partition_broadcast(128), writes=[B_g])
            for t in range(min(2, NT)):
                sp.dma(ds_x[t % 2], xts[t % 2][:], src_d[t * 128:(t + 1) * 128, :], reads=[B_src], writes=[B_xt[t % 2]])
            def stage_a(t):
                s = t % 2
                xt = xts[s]
                act.op(lambda e: e.activation(out=junk[:], in_=xt[:], func=AF.Square,
                                              accum_out=nst[:, 4 * t:4 * t + 1]),
                       reads=[B_xt[s]], writes=[B_junk, B_st[t]])
                act.op(lambda e: e.activation(out=nst[:, 4 * t + 1:4 * t + 2], in_=nst[:, 4 * t:4 * t + 1],
                                              func=AF.Sqrt, scale=1.0 / D, bias=EPS),
                       reads=[B_st[t]], writes=[B_st[t]])
                dve.op(lambda e: e.reciprocal(out=nst[:, 4 * t + 2:4 * t + 3], in_=nst[:, 4 * t + 1:4 * t + 2]),
                       reads=[B_st[t]], writes=[B_st[t]])
                xn = xns[s]
                dve.op(lambda e: e.scalar_tensor_tensor(out=xn[:], in0=xt[:], scalar=nst[:, 4 * t + 2:4 * t + 3],
                                                        in1=gbc[:], op0=ALU.mult, op1=ALU.mult),
                       reads=[B_xt[s], B_st[t], B_g], writes=[B_xnk[s]])
                if t + 2 < NT:
                    sp.dma(ds_x[s], xts[s][:], src_d[(t + 2) * 128:(t + 3) * 128, :], reads=[B_src],
                           writes=[B_xt[s]])

            def stage_b(t):
                s = t % 2
                xn = xns[s]
                for q4 in range(4):
                    bank = (t * 4 + q4) % 8
                    pv = PS[bank][:].bitcast(BF16)
                    for j in range(4):
                        c = q4 * 4 + j
                        pe.op(lambda e: e.transpose(out=pv[:, j * 128:(j + 1) * 128],
                                                    in_=xn[:, c * 128:(c + 1) * 128], identity=ident[:]),
                              reads=[B_xnk[s], B_const], writes=[PB[bank]])
                    evac(xnT[:, q4 * 4:q4 * 4 + 4, t * 128:(t + 1) * 128],
                         pv[:, 0:512].rearrange("p (a b) -> p a b", a=4),
                         reads=[PB[bank]], parts=[B_xn[t]])

            stage_a(0)
            for t in range(NT):
                if t + 1 < NT:
                    stage_a(t + 1)
                stage_b(t)

    L0 = nc.sbuf_tensor("xnT", [128, KC, T], BF16)
    xnT = L0.__enter__()
    B_xn = [Buf("xnT%d" % t) for t in range(NT)]
    st_ctx = nc.sbuf_tensor("stg", [128, 4, T], BF16)
    stg = st_ctx.__enter__()
    if stop >= 3:
        vln_ctx = nc.sbuf_tensor("vln", [128, NT, D], BF16)
        vln = vln_ctx.__enter__()
        B_vln = [Buf() for _ in range(NT)]
        cst_ctx = nc.sbuf_tensor("sgc", [128, 3, D], F32)
        sgc = cst_ctx.__enter__()
        wsT_ctx = nc.sbuf_tensor("wsT", [128, 8, 128], BF16)
        wsT = wsT_ctx.__enter__()
        sb_ctx = nc.sbuf_tensor("sgb", [128, 16], F32)
        sgb = sb_ctx.__enter__()
    norm_phase(x_d, Buf(), 0, xnT, B_xn, "_a")
    if dbg and stop == 1:
        o = nc.dram_tensor("dbg_xnT", [128, KC * T], BF16, kind="ExternalOutput").ap()
        dsd = K.dsem("dbg0")
        b0 = Buf()
        sp.dma(dsd, o, xnT[:].rearrange("p a b -> p (a b)"), reads=B_xn, writes=[b0])
        dbg_out["xnT"] = b0

    if stop >= 2:
        B_stg = [Buf() for _ in range(4)]
        ds_stg = [K.dsem("stg%d" % i) for i in range(4)]
        sti = [0]

        def qk_job(kind, g):
            col0 = (0 if kind == "q" else 2048) + g * 512

            def comp(slot):
                for j in range(4):
                    hm = g * 4 + j
                    si = sti[0] % 4
                    sti[0] += 1
                    for half in range(2):
                        bank = (hm * 2 + half) % 8
                        for c in range(KC):
                            pe.op(lambda e: e.matmul(out=PS[bank][:, :], lhsT=WB[slot][:, c, j * 128:(j + 1) * 128],
                                                     rhs=xnT[:, c, half * 512:(half + 1) * 512],
                                                     start=(c == 0), stop=(c == KC - 1)),
                                  reads=[B_wb[slot]] + B_xn[half * 4:half * 4 + 4], writes=[PB[bank]])
                        evac(stg[:, si, half * 512:(half + 1) * 512], PS[bank][:, :], reads=[PB[bank]],
                             writes=[B_stg[si]], scale=(128.0 ** -0.5 if kind == "q" else None))
                    if kind == "q":
                        sp.dma(ds_stg[si], qT_d[:, hm * T:(hm + 1) * T], stg[:, si, :], reads=[B_stg[si]],
                               writes=[B_qT[hm]])
                    else:
                        sp.dma(ds_stg[si], kvl[hm // 2][0:128, (hm % 2) * T:(hm % 2 + 1) * T], stg[:, si, :], reads=[B_stg[si]],
                               parts=[B_kvl[hm // 2]])
            return (wload(w_in, col0, 512), comp)

        def tok_job(col0, sink):
            def comp(slot):
                for t in range(NT):
                    bank = t % 8
                    for c in range(KC):
                        pe.op(lambda e: e.matmul(out=PS[bank][:, :], lhsT=xnT[:, c, t * 128:(t + 1) * 128],
                                                 rhs=WB[slot][:, c, 0:512], start=(c == 0), stop=(c == KC - 1)),
                              reads=[B_wb[slot], B_xn[t]], writes=[PB[bank]])
                    sink(t, bank)
            return (wload(w_in, col0, 512), comp)

        def v_sink(g):
            def f(t, bank):
                si = sti[0] % 4
                sti[0] += 1
                evac(stg[:, si, 0:512], PS[bank][:, :], reads=[PB[bank]], writes=[B_stg[si]])
                for hh in range(2):
                    sp.dma(ds_stg[si], kvl[2 * g + hh][128:256, t * 256:(t + 1) * 256], stg[:, si, hh * 256:(hh + 1) * 256],
                           reads=[B_stg[si]], parts=[B_kvl[2 * g + hh]])
            return f

        def za_sink(g):
            def f(t, bank):
                si = sti[0] % 4
                sti[0] += 1
                act.op(lambda e: e.activation(out=stg[:, si, 0:512], in_=PS[bank][:, :], func=AF.Silu),
                       reads=[PB[bank]], writes=[B_stg[si]])
                sp.dma(ds_stg[si], sza_d[t * 128:(t + 1) * 128, g * 512:(g + 1) * 512], stg[:, si, 0:512],
                       reads=[B_stg[si]], parts=[B_sza])
            return f

        def ag_job(g, heads=None):
            def comp(slot):
                for h in (heads if heads is not None else (2 * g, 2 * g + 1)):
                    ds_ag = K.dsem("ag_kv%d" % h, step=1)
                    pool.all_gather(ds_ag, kvg[h], kvl[h], reads=[B_kvl[h]], writes=[B_kvg[h]], qos="P2")
            return (None, comp)
        kq = [[qk_job("k", g), tok_job(4096 + g * 512, v_sink(g)), qk_job("q", g)] for g in range(4)]
        if stop < 3:
            jobs = []
            for g in range(4):
                jobs += kq[g] + [ag_job(g)]
            run_stream(jobs)
        if dbg and stop == 2:
            dbg_dump("qT", qT_d, [128, 16 * T], BF16, B_qT)
            dbg_dump("kvg0", kvg[0], [1024, 2048], BF16, [B_kvg[0]])
            dbg_dump("kvg7", kvg[7], [1024, 2048], BF16, [B_kvg[7]])

    if stop >= 3:
        B_sgc = Buf()
        B_ws = Buf()
        ds_s = K.dsem("sgc")
        ds_ws = K.dsem("wsT")
        pool.dma(ds_ws, wsT[:], sgu_wT, writes=[B_ws])
        sp.dma(ds_s, sgc[:, 0, :], vecs[2, :].partition_broadcast(128), parts=[B_sgc])
        sp.dma(ds_s, sgc[:, 1, :], vecs[3, :].partition_broadcast(128), parts=[B_sgc])
        sp.dma(ds_s, sgb[:, 0:8], sgu_b, parts=[B_sgc])
        dve.op(lambda e: e.memset(wsT[64:128, :, 0:64], 0.0), reads=[B_ws], writes=[B_ws])
        for g in range(8):
            pe.op(lambda e: e.matmul(out=PS[0][:, g:g + 1], lhsT=wsT[:, g, :], rhs=ones_b[:, 0:1],
                                     start=True, stop=True), reads=[B_ws, B_const], writes=[PB[0]])
        dve.op(lambda e: e.tensor_copy(out=sgb[:, 8:16], in_=PS[0][:, 0:8]), reads=[PB[0], B_sgc], writes=[B_sgc])
        for g in range(8):
            dve.op(lambda e: e.tensor_scalar(out=sgc[:, 2, g * 256:(g + 1) * 256], in0=sgc[:, 1, g * 256:(g + 1) * 256],
                                             scalar1=sgb[:, 8 + g:9 + g], scalar2=sgb[:, g:g + 1],
                                             op0=ALU.mult, op1=ALU.add), reads=[B_sgc], writes=[B_sgc])
        za = [tok_job(6144 + g * 512, za_sink(g)) for g in range(4)]
        jobs = list(kq[0])
        for g in range(1, 4):
            jobs += [kq[g][0], ag_job(g - 1, heads=(2 * g - 2,)), kq[g][1], ag_job(g - 1, heads=(2 * g - 1,)), kq[g][2]]
        jobs += [za[0], ag_job(3, heads=(6,)), za[1], ag_job(3, heads=(7,)), za[2], za[3]]
        run_stream(jobs)
        with nc.sbuf_tensor("bst", [128, NT, 4, 6], F32) as bst, nc.sbuf_tensor("mv", [128, NT, 4], F32) as mv, \
             nc.sbuf_tensor("ug", [128, NT, 512], F32) as ug, nc.sbuf_tensor("tmpb", [128, 4, 512], F32) as tmpb:
            B_bst = [Buf() for _ in range(NT)]

            def vb_job(g):
                def comp(slot):
                    for t in range(NT):
                        bank = (g * NT + t) % 8
                        for c in range(KC):
                            pe.op(lambda e: e.matmul(out=PS[bank][:, :], lhsT=xnT[:, c, t * 128:(t + 1) * 128],
                                                     rhs=WB[slot][:, c, 0:512], start=(c == 0), stop=(c == KC - 1)),
                                  reads=[B_wb[slot], B_xn[t]], writes=[PB[bank]])
                        act.op(lambda e: e.activation(out=vln[:, t, g * 512:(g + 1) * 512], in_=PS[bank][:, :],
                                                      func=AF.Gelu_apprx_tanh), reads=[PB[bank]], parts=[B_vln[t]])
                        dve.op(lambda e: e.bn_stats(out=bst[:, t, g, :], in_=vln[:, t, g * 512:(g + 1) * 512]),
                               reads=[B_vln[t]], parts=[B_bst[t]])
                return (wload(w_in, 10240 + g * 512, 512), comp)
            run_stream([vb_job(g) for g in range(4)])
            for t in range(NT):
                dve.op(lambda e: e.bn_aggr(out=mv[:, t, 0:2], in_=bst[:, t, :, :].rearrange("p a b -> p (a b)")),
                       reads=[B_bst[t]], writes=[B_bst[t]])
                act.op(lambda e: e.activation(out=mv[:, t, 2:3], in_=mv[:, t, 1:2], func=AF.Sqrt, bias=EPS, scale=1.0),
                       reads=[B_bst[t]], writes=[B_bst[t]])
                dve.op(lambda e: e.reciprocal(out=mv[:, t, 3:4], in_=mv[:, t, 2:3]), reads=[B_bst[t]], writes=[B_bst[t]])
                dve.op(lambda e: e.tensor_scalar(out=vln[:, t, :], in0=vln[:, t, :], scalar1=mv[:, t, 0:1],
                                                 scalar2=mv[:, t, 3:4], op0=ALU.subtract, op1=ALU.mult),
                       reads=[B_vln[t], B_bst[t]], writes=[B_vln[t]])
            B_ug = [Buf() for _ in range(NT)]
            B_tm = [Buf() for _ in range(4)]
            jobs = []
            for cg in range(4):
                def u_sink(t, bank):
                    act.op(lambda e: e.activation(out=ug[:, t, :], in_=PS[bank][:, :], func=AF.Gelu_apprx_tanh),
                           reads=[PB[bank]], writes=[B_ug[t]])

                def z_sink(t, bank, cg=cg):
                    mb = (bank + 4) % 8
                    for gg in range(2):
                        g = cg * 2 + gg
                        pe.op(lambda e: e.matmul(out=PS[mb][:, gg * 256:(gg + 1) * 256], lhsT=wsT[:, g, :],
                                                 rhs=vln[:, t, g * 256:(g + 1) * 256], start=True, stop=True),
                              reads=[B_ws, B_vln[t]], writes=[PB[mb]])
                    cs = slice(cg * 512, (cg + 1) * 512)
                    act.op(lambda e: e.activation(out=tmpb[:, 0, :], in_=PS[bank][:, :], func=AF.Silu),
                           reads=[PB[bank]], writes=[B_tm[0]])
                    dve.op(lambda e: e.tensor_tensor(out=tmpb[:, 1, :], in0=PS[mb][:, :], in1=sgc[:, 0, cs], op=ALU.mult),
                           reads=[PB[mb], B_sgc], writes=[B_tm[1]])
                    dve.op(lambda e: e.tensor_tensor(out=tmpb[:, 2, :], in0=tmpb[:, 1, :], in1=sgc[:, 2, cs], op=ALU.add),
                           reads=[B_tm[1], B_sgc], writes=[B_tm[2]])
                    dve.op(lambda e: e.tensor_tensor(out=tmpb[:, 3, :], in0=tmpb[:, 2, :], in1=ug[:, t, :], op=ALU.mult),
                           reads=[B_tm[2], B_ug[t]], writes=[B_tm[3]])
                    si = sti[0] % 4
                    sti[0] += 1
                    dve.op(lambda e: e.tensor_tensor(out=stg[:, si, 0:512], in0=tmpb[:, 3, :], in1=tmpb[:, 0, :], op=ALU.mult),
                           reads=[B_tm[3], B_tm[0]], writes=[B_stg[si]])
                    sp.dma(ds_stg[si], ymix_d[t * 128:(t + 1) * 128, 2048 + cg * 512:2048 + (cg + 1) * 512],
                           stg[:, si, 0:512], reads=[B_stg[si]], parts=[B_ymix])
                jobs.append(tok_job(8192 + cg * 512, u_sink))
                jobs.append(tok_job(12288 + cg * 512, z_sink))
            run_stream(jobs)
            K.barrier()
        sb_ctx.__exit__(None, None, None)
        wsT_ctx.__exit__(None, None, None)
        cst_ctx.__exit__(None, None, None)
        vln_ctx.__exit__(None, None, None)
        if dbg and stop == 3:
            dbg_dump("sza", sza_d, [T, D], BF16, [B_sza])
            dbg_dump("ymix", ymix_d, [T, 4096], BF16, [B_ymix])
    st_ctx.__exit__(None, None, None)
    L0.__exit__(None, None, None)
    K.barrier()

    if stop >= 4:
        NPT = 5
        LOOK = 2
        with nc.sbuf_tensor("qh", [128, 2, 2, T], BF16) as qh, nc.sbuf_tensor("kh", [128, 2, 4, 2, T], BF16) as kh, \
             nc.sbuf_tensor("vh", [128, 2, 4, NT, 256], BF16) as vh, nc.sbuf_tensor("pT", [128, NPT, 2, 256], BF16) as pT, \
             nc.sbuf_tensor("ghn", [128, D], F32) as ghn, nc.sbuf_tensor("lp", [128, 4, 128], F32) as lp, \
             nc.sbuf_tensor("lsc", [128, 16], F32) as lsc, nc.sbuf_tensor("ljk", [128, 128], F32) as ljk, \
             nc.sbuf_tensor("ep", [128, 2, 4, 256], F32) as ep, nc.sbuf_tensor("es", [128, 2, 16], F32) as es, \
             nc.sbuf_tensor("szt", [128, 4, 256], BF16) as szt, nc.sbuf_tensor("yst", [128, 2, 256], BF16) as yst, \
             nc.sbuf_tensor("ejk", [128, 256], BF16) as ejk:
            B_c4 = Buf()
            ds4 = K.dsem("c4")
            sp.dma(ds4, ghn[:], vecs[1, :].partition_broadcast(128), writes=[B_c4])
            sp.dma(ds4, lp[:].rearrange("p a b -> p (a b)"), lam_p.partition_broadcast(128), parts=[B_c4])
            dve.op(lambda e: e.tensor_scalar(out=ghn[:], in0=ghn[:], scalar1=1.0 - LAM_INIT, scalar2=None, op0=ALU.mult),
                   reads=[B_c4], writes=[B_c4])
            for i in range(2):
                dve.op(lambda e: e.tensor_tensor(out=ljk[:], in0=lp[:, 2 * i, :], in1=lp[:, 2 * i + 1, :], op=ALU.mult),
                       reads=[B_c4], writes=[B_c4])
                dve.op(lambda e: e.reduce_sum(out=lsc[:, i:i + 1], in_=ljk[:], axis=AX.X), reads=[B_c4], writes=[B_c4])
            act.op(lambda e: e.activation(out=lsc[:, 2:4], in_=lsc[:, 0:2], func=AF.Exp), reads=[B_c4], writes=[B_c4])
            dve.op(lambda e: e.scalar_tensor_tensor(out=lsc[:, 4:5], in0=lsc[:, 3:4], scalar=-LAM_INIT, in1=lsc[:, 2:3],
                                                    op0=ALU.add, op1=ALU.subtract), reads=[B_c4], writes=[B_c4])
            dve.op(lambda e: e.memset(lsc[:, 5:6], -0.5), reads=[B_c4], writes=[B_c4])
            B_qh = [Buf(), Buf()]
            B_kh = [[Buf() for _ in range(4)] for _ in range(2)]
            B_vh = [[Buf() for _ in range(4)] for _ in range(2)]
            B_pT = [Buf() for _ in range(NPT)]
            ds_q = [K.dsem("q0"), K.dsem("q1")]
            ds_k = [[K.dsem("k%d%d" % (a_, b_)) for b_ in range(4)] for a_ in range(2)]
            ds_v = [[K.dsem("v%d%d" % (a_, b_)) for b_ in range(4)] for a_ in range(2)]
            ds_sz = [K.dsem("sz%d" % i_) for i_ in range(4)]
            ds_y = [K.dsem("y0"), K.dsem("y1")]
            B_ep = [Buf(), Buf()]
            B_es = [Buf(), Buf()]
            B_szt = [Buf() for _ in range(4)]

            def load_szt(h, qb):
                for qs in range(2):
                    z_ = ((h * 4 + qb) % 2) * 2 + qs
                    t = qb * 2 + qs
                    sp.dma(ds_sz[z_], szt[:, z_, :], sza_d[t * 128:(t + 1) * 128, h * 256:(h + 1) * 256],
                           reads=[B_sza], writes=[B_szt[z_]])
            B_yst = [Buf(), Buf()]
            B_ejk = Buf()
            pti = [0]
            sbi = [0]
            epi = [0]
            SB = [5, 6, 7]
            OBK = [[0, 1], [3, 4]]
            SUMB = 2
            B_sum = Buf()

            def load_head(h):
                hp = h % 2
                srcs = [(kvl[h], 0, B_kvl[h])] + [(kvg[h], 256 * r, B_kvg[h]) for r in range(3)]
                sp.dma(ds_q[hp], qh[:, hp, :, :].rearrange("p m t -> p (m t)"), qT_d[:, h * 2 * T:(h + 1) * 2 * T],
                       reads=B_qT[2 * h:2 * h + 2], writes=[B_qh[hp]])
                for si_, (src, row0, B_src) in enumerate(srcs):
                    sp.dma(ds_k[hp][si_], kh[:, hp, si_, :, :].rearrange("p m t -> p (m t)"),
                           src[row0:row0 + 128, :], reads=[B_src], writes=[B_kh[hp][si_]])
                    sp.dma(ds_v[hp][si_], vh[:, hp, si_, :, :].rearrange("p t c -> p (t c)"),
                           src[row0 + 128:row0 + 256, :], reads=[B_src], writes=[B_vh[hp][si_]])

            allv = []
            for h in range(8):
                for qb in range(4):
                    vis = []
                    for kt in range(2 * qb + 2):
                        qsd = kt - 2 * qb
                        vis.append((0, kt, max(qsd, 0) * 128, qsd >= 0, None))
                    for r in range(3):
                        for kt in range(NT):
                            vis.append((1 + r, kt, 0, False, r))
                    for i_, v_ in enumerate(vis):
                        allv.append((h, qb, v_, i_ == 0, i_ == len(vis) - 1))

            def emit_scores(h, qb, v_):
                (si_, kt, qcol0, diag, r) = v_
                hp = h % 2
                bank = SB[sbi[0] % len(SB)]
                sbi[0] += 1
                for m in range(2):
                    pe.op(lambda e: e.matmul(out=PS[bank][:, m * 256 + qcol0:(m + 1) * 256],
                                             lhsT=kh[:, hp, si_, m, kt * 128:(kt + 1) * 128],
                                             rhs=qh[:, hp, m, qb * 256 + qcol0:qb * 256 + 256], start=True, stop=True),
                          reads=[B_kh[hp][si_], B_qh[hp]], writes=[PB[bank]])
                ps_ = pti[0] % NPT
                pti[0] += 1
                src_ap = PS[bank][:, :].rearrange("p (m q) -> p m q", m=2)[:, :, qcol0:256]
                if r is None:
                    act.op(lambda e: e.activation(out=pT[:, ps_, :, qcol0:256], in_=src_ap, func=AF.Exp),
                           reads=[PB[bank]], writes=[B_pT[ps_]])
                else:
                    act.op(lambda e: e.activation(out=pT[:, ps_, :, qcol0:256], in_=src_ap, func=AF.Exp,
                                                  bias=masks[:, r:r + 1], scale=1.0),
                           reads=[PB[bank], B_const], writes=[B_pT[ps_]])
                if diag:
                    act.op(lambda e: e.activation(out=pT[64:128, ps_, :, qcol0:qcol0 + 64], in_=pT[64:128, ps_, :, qcol0:qcol0 + 64],
                                                  func=AF.Copy, scale=0.0), reads=[B_pT[ps_]], writes=[B_pT[ps_]])
                return ps_

            def emit_pv(h, qb, v_, ps_, first):
                (si_, kt, qcol0, diag, r) = v_
                hp = h % 2
                par = (h * 4 + qb) % 2
                for qs in range(qcol0 // 128, 2):
                    ob = OBK[par][qs]
                    for m in range(2):
                        pe.op(lambda e: e.matmul(out=PS[ob][:, m * 256:(m + 1) * 256],
                                                 lhsT=pT[:, ps_, m, qs * 128:(qs + 1) * 128], rhs=vh[:, hp, si_, kt, :],
                                                 start=(first and m == 0), stop=False, skip_group_check=True),
                              reads=[B_pT[ps_], B_vh[hp][si_]], writes=[PB[ob]])
                        c_ = par * 4 + qs * 2 + m
                        pe.op(lambda e: e.matmul(out=PS[SUMB][:, c_:c_ + 1],
                                                 lhsT=pT[:, ps_, m, qs * 128:(qs + 1) * 128], rhs=ones_b[:, 0:1],
                                                 start=(first and m == 0 and qs == 0), stop=False, skip_group_check=True),
                              reads=[B_pT[ps_], B_const], writes=[B_sum])

            def emit_epilogue(h, qb):
                par = (h * 4 + qb) % 2
                es_ = []
                for qs in range(2):
                    e_ = epi[0] % 2
                    epi[0] += 1
                    es_.append(e_)
                    dve.op(lambda e: e.reciprocal(out=es[:, e_, 0:2], in_=PS[SUMB][:, par * 4 + qs * 2:par * 4 + qs * 2 + 2]),
                           reads=[B_sum], writes=[B_es[e_]])
                for qs in range(2):
                    ob = OBK[par][qs]
                    e_ = es_[qs]
                    t = qb * 2 + qs
                    z_ = par * 2 + qs
                    dve.op(lambda e: e.tensor_tensor(out=es[:, e_, 2:3], in0=es[:, e_, 1:2], in1=lsc[:, 4:5], op=ALU.mult),
                           reads=[B_es[e_], B_c4], writes=[B_es[e_]])
                    dve.op(lambda e: e.tensor_scalar(out=ep[:, e_, 0, :], in0=PS[ob][:, 0:256], scalar1=es[:, e_, 0:1],
                                                     scalar2=None, op0=ALU.mult), reads=[PB[ob], B_es[e_]], writes=[B_ep[e_]])
                    dve.op(lambda e: e.scalar_tensor_tensor(out=ep[:, e_, 1, :], in0=PS[ob][:, 256:512], scalar=es[:, e_, 2:3],
                                                            in1=ep[:, e_, 0, :], op0=ALU.mult, op1=ALU.add),
                           reads=[PB[ob], B_es[e_], B_ep[e_]], writes=[B_ep[e_]])
                    dve.op(lambda e: e.scalar_tensor_tensor(out=ep[:, e_, 3, :], in0=ep[:, e_, 1, :], scalar=1.0, in1=ep[:, e_, 1, :],
                                                            op0=ALU.mult, op1=ALU.mult, accum_out=es[:, e_, 4:5]),
                           reads=[B_ep[e_]], writes=[B_es[e_]])
                    dve.op(lambda e: e.tensor_scalar(out=es[:, e_, 5:6], in0=es[:, e_, 4:5], scalar1=1.0 / 256, scalar2=EPS,
                                                     op0=ALU.mult, op1=ALU.add), reads=[B_es[e_]], writes=[B_es[e_]])
                    pool.op(lambda e: e.tensor_tensor(out=es[:, e_, 6:7], in0=es[:, e_, 5:6], in1=lsc[:, 5:6], op=ALU.pow),
                            reads=[B_es[e_], B_c4], writes=[B_es[e_]])
                    dve.op(lambda e: e.scalar_tensor_tensor(out=ep[:, e_, 2, :], in0=ep[:, e_, 1, :], scalar=es[:, e_, 6:7],
                                                            in1=ghn[:, h * 256:(h + 1) * 256], op0=ALU.mult, op1=ALU.mult),
                           reads=[B_ep[e_], B_es[e_], B_c4], writes=[B_ep[e_]])
                    dve.op(lambda e: e.tensor_tensor(out=yst[:, e_, :], in0=ep[:, e_, 2, :], in1=szt[:, z_, :], op=ALU.mult),
                           reads=[B_ep[e_], B_szt[z_]], writes=[B_yst[e_]])
                    sp.dma(ds_y[e_], ymix_d[t * 128:(t + 1) * 128, h * 256:(h + 1) * 256], yst[:, e_, :],
                           reads=[B_yst[e_]], parts=[B_ymix])

            load_head(0)
            pend = []

            def drain_one():
                (h2, qb2, v2, p2, f2, l2) = pend.pop(0)
                if f2:
                    load_szt(h2, qb2)
                if qb2 == 0 and f2 and h2 + 1 < 8:
                    load_head(h2 + 1)
                emit_pv(h2, qb2, v2, p2, f2)
                if l2:
                    emit_epilogue(h2, qb2)
            for idx, (h, qb, v_, first, last) in enumerate(allv):
                ps_ = emit_scores(h, qb, v_)
                pend.append((h, qb, v_, ps_, first, last))
                if len(pend) > LOOK:
                    drain_one()
            while pend:
                drain_one()
            K.barrier()
        if dbg and stop == 4:
            dbg_dump("ymix", ymix_d, [T, 4096], BF16, [B_ymix])

    if stop >= 5:
        with nc.sbuf_tensor("ymT", [128, 32, T], BF16) as ymT, nc.sbuf_tensor("yt", [128, 2, 4096], BF16) as yt, \
             nc.sbuf_tensor("wo0", [128, 32, 512], BF16) as wo0, nc.sbuf_tensor("wo1", [128, 32, 512], BF16) as wo1, \
             nc.sbuf_tensor("xr", [128, 4, 512], F32) as xr:
            WO = [wo0, wo1]
            B_wo = [Buf(), Buf()]
            ds_wo = [K.dsem("wo0"), K.dsem("wo1")]
            B_ymT = [Buf() for _ in range(NT)]
            B_yt = [Buf(), Buf()]
            ds_yt = [K.dsem("yt0"), K.dsem("yt1")]
            B_xr = [Buf() for _ in range(4)]
            ds_xr = [K.dsem("xr%d" % i) for i in range(4)]
            ds_xo = [K.dsem("xo%d" % i) for i in range(4)]

            def wo_load(cg, slot):
                pool.dma(ds_wo[slot], WO[slot][:, :, :],
                         w_out.rearrange("(kc p) n -> p kc n", p=128)[:, :, cg * 512:(cg + 1) * 512], writes=[B_wo[slot]])
            wo_load(0, 0)
            wo_load(1, 1)
            for t in range(NT):
                s = t % 2
                sp.dma(ds_yt[s], yt[:, s, :], ymix_d[t * 128:(t + 1) * 128, :], reads=[B_ymix], writes=[B_yt[s]])
                for q8 in range(8):
                    bank = (t * 8 + q8) % 8
                    pv = PS[bank][:].bitcast(BF16)
                    for j in range(4):
                        c = q8 * 4 + j
                        pe.op(lambda e: e.transpose(out=pv[:, j * 128:(j + 1) * 128], in_=yt[:, s, c * 128:(c + 1) * 128],
                                                    identity=ident[:]), reads=[B_yt[s], B_const], writes=[PB[bank]])
                    evac(ymT[:, q8 * 4:q8 * 4 + 4, t * 128:(t + 1) * 128], pv[:, 0:512].rearrange("p (a b) -> p a b", a=4),
                         reads=[PB[bank]], parts=[B_ymT[t]])
            xi = [0]
            for cg in range(4):
                slot = cg % 2
                for t in range(NT):
                    bank = t % 8
                    k_ = xi[0] % 4
                    xi[0] += 1
                    sp.dma(ds_xr[k_], xr[:, k_, :], x_d[t * 128:(t + 1) * 128, cg * 512:(cg + 1) * 512], writes=[B_xr[k_]])
                    for c in range(32):
                        pe.op(lambda e: e.matmul(out=PS[bank][:, :], lhsT=ymT[:, c, t * 128:(t + 1) * 128],
                                                 rhs=WO[slot][:, c, :], start=(c == 0), stop=(c == 31)),
                              reads=[B_wo[slot], B_ymT[t]], writes=[PB[bank]])
                    dve.op(lambda e: e.tensor_tensor(out=xr[:, k_, :], in0=PS[bank][:, :], in1=xr[:, k_, :], op=ALU.add),
                           reads=[PB[bank], B_xr[k_]], writes=[B_xr[k_]])
                    sp.dma(ds_xo[k_], x1_d[t * 128:(t + 1) * 128, cg * 512:(cg + 1) * 512], xr[:, k_, :],
                           reads=[B_xr[k_]], parts=[B_x1])
                if cg + 2 < 4:
                    wo_load(cg + 2, slot)
            K.barrier()
        if dbg and stop == 5:
            dbg_dump("x1", x1_d, [T, D], F32, [B_x1])

    if stop >= 6:
        sz_ctx = nc.sbuf_tensor("sz", [128, NCH, T], BF16)
        sz = sz_ctx.__enter__()
        xp_ctx = nc.sbuf_tensor("xpre", [128, NCH, T + 4], BF16)
        xpre = xp_ctx.__enter__()
        B_sz = [Buf() for _ in range(NCH)]
        B_xp = [Buf() for _ in range(NCH)]
        hst_ctx = nc.sbuf_tensor("hst", [128, NCH, 3], BF16)
        hst = hst_ctx.__enter__()
        hal_ctx = nc.sbuf_tensor("hal", [128, 4, NCH * 3], BF16)
        hal = hal_ctx.__enter__()
        hac_ctx = nc.sbuf_tensor("hac", [128, 2, NCH * 3], F32)
        hac = hac_ctx.__enter__()
        xn_ctx = nc.sbuf_tensor("xnT1", [128, KC, T], BF16)
        xnT1 = xn_ctx.__enter__()
        B_xn1 = [Buf() for _ in range(NT)]
        norm_phase(x1_d, B_x1, 4, xnT1, B_xn1, "_b")

        def l1_job(kind, g):
            col0 = (CW if kind == "z" else 0) + g * 384

            def comp(slot):
                for j in range(3):
                    n = g * 3 + j
                    for half in range(2):
                        bank = (n * 2 + half) % 8
                        for c in range(KC):
                            pe.op(lambda e: e.matmul(out=PS[bank][:, :], lhsT=WB[slot][:, c, j * 128:(j + 1) * 128],
                                                     rhs=xnT1[:, c, half * 512:(half + 1) * 512],
                                                     start=(c == 0), stop=(c == KC - 1)),
                                  reads=[B_wb[slot]] + B_xn1[half * 4:half * 4 + 4], writes=[PB[bank]])
                        if kind == "z":
                            act.op(lambda e: e.activation(out=sz[:, n, half * 512:(half + 1) * 512], in_=PS[bank][:, :],
                                                          func=AF.Silu), reads=[PB[bank]], parts=[B_sz[n]])
                        else:
                            evac(xpre[:, n, 3 + half * 512:3 + (half + 1) * 512], PS[bank][:, :], reads=[PB[bank]],
                                 parts=[B_xp[n]])
            return (wload(cw_in, col0, 384), comp)
        run_stream([l1_job("x", g) for g in range(7)])
        B_h = Buf()
        B_hl = Buf()
        B_hg = Buf()
        ds_h = K.dsem("halo")
        ds_hag = K.dsem("halo_ag", step=1)
        dve.op(lambda e: e.tensor_copy(out=hst[:], in_=xpre[:, :, T:T + 3]), reads=B_xp, writes=[B_h])
        sp.dma(ds_h, halo_l, hst[:].rearrange("p a b -> p (a b)"), reads=[B_h], writes=[B_hl])
        pool.all_gather(ds_hag, halo_g, halo_l, reads=[B_hl], writes=[B_hg])
        run_stream([l1_job("z", g) for g in range(7)])
        B_h2 = Buf()
        sp.dma(ds_h, hal[:], halo_g.rearrange("(r p) f -> p r f", p=128), reads=[B_hg], writes=[B_h2])
        dve.op(lambda e: e.tensor_scalar(out=hac[:, 0, :], in0=hal[:, 0, :], scalar1=masks[:, 4:5], scalar2=None, op0=ALU.mult),
               reads=[B_h2, B_const], writes=[B_h])
        for r in range(1, 4):
            dve.op(lambda e: e.scalar_tensor_tensor(out=hac[:, r % 2, :], in0=hal[:, r, :], scalar=masks[:, 4 + r:5 + r],
                                                    in1=hac[:, (r - 1) % 2, :], op0=ALU.mult, op1=ALU.add),
                   reads=[B_h2, B_h, B_const], writes=[B_h])
        dve.op(lambda e: e.tensor_copy(out=xpre[:, :, 0:3], in_=hac[:, 1, :].rearrange("p (a b) -> p a b", b=3)),
               reads=[B_h], parts=B_xp)
        K.barrier()
        xn_ctx.__exit__(None, None, None)
        hac_ctx.__exit__(None, None, None)
        hal_ctx.__exit__(None, None, None)
        hst_ctx.__exit__(None, None, None)
        if dbg and stop == 6:
            o = nc.dram_tensor("dbg_xpre", [128, NCH * (T + 4)], BF16, kind="ExternalOutput").ap()
            o2 = nc.dram_tensor("dbg_sz", [128, NCH * T], BF16, kind="ExternalOutput").ap()
            dsd = K.dsem("dbgl1")
            b = Buf()
            sp.dma(dsd, o, xpre[:].rearrange("p a b -> p (a b)"), reads=B_xp, writes=[b])
            sp.dma(dsd, o2, sz[:].rearrange("p a b -> p (a b)"), reads=B_sz, writes=[b])
            dbg_out["l1"] = b

    if stop >= 7:
        xc_ctx = nc.sbuf_tensor("xcb", [128, NCH, T], BF16)
        xcb = xc_ctx.__enter__()
        B_xc = [Buf() for _ in range(NCH)]
        with nc.sbuf_tensor("cwt", [128, NCH, 4], F32) as cwt, nc.sbuf_tensor("chs", [128, 10, NCH], F32) as chs, \
             nc.sbuf_tensor("idf", [128, 128], F32) as idf, nc.sbuf_tensor("dg", [128, 2, 4, 128], BF16) as dg, \
             nc.sbuf_tensor("gw", [128, 2, 2, 4, 128], BF16) as gw, nc.sbuf_tensor("cst", [128, 2, T], BF16) as cst, \
             nc.sbuf_tensor("stt", [128, 2, NCH], F32) as stt, nc.sbuf_tensor("sta", [128, 4, 2 * NCH], F32) as sta, \
             nc.sbuf_tensor("hin", [128, 6, NCH], F32) as hin:
            B_cc = Buf()
            ds_cc = K.dsem("cc")
            sp.dma(ds_cc, cwt[:], conv_w, writes=[B_cc])
            sp.dma(ds_cc, chs[:, 0:4, :], chv, parts=[B_cc])
            sp.dma(ds_cc, idf[:], ident_d, parts=[B_cc])
            dve.op(lambda e: e.memset(cst[:, 0, :], 0.0), parts=[B_cc])
            dve.op(lambda e: e.memset(cst[:, 1, :], 0.5), parts=[B_cc])
            act.op(lambda e: e.activation(out=chs[:, 4, :], in_=chs[:, 3, :], func=AF.Exp, scale=-1.0), reads=[B_cc], writes=[B_cc])
            act.op(lambda e: e.activation(out=chs[:, 5, :], in_=chs[:, 4, :], func=AF.Ln, bias=1.0, scale=1.0), reads=[B_cc], writes=[B_cc])
            dve.op(lambda e: e.tensor_scalar(out=chs[:, 6, :], in0=chs[:, 5, :], scalar1=-4.0, scalar2=None, op0=ALU.mult),
                   reads=[B_cc], writes=[B_cc])
            dve.op(lambda e: e.tensor_scalar(out=chs[:, 7:9, :], in0=chs[:, 1:3, :], scalar1=0.5, scalar2=None, op0=ALU.mult),
                   reads=[B_cc], writes=[B_cc])
            B_dg = [Buf(), Buf()]
            for n in range(NCH):
                s = n % 2
                for tau in range(4):
                    dve.op(lambda e: e.tensor_scalar(out=dg[:, s, tau, :], in0=idf[:], scalar1=cwt[:, n, tau:tau + 1], scalar2=None,
                                                     op0=ALU.mult), reads=[B_cc], parts=[B_dg[s]] if tau else (), writes=() if tau else [B_dg[s]])
                for half in range(2):
                    bank = (n * 2 + half) % 8
                    for tau in range(4):
                        pe.op(lambda e: e.matmul(out=PS[bank][:, :], lhsT=dg[:, s, tau, :],
                                                 rhs=xpre[:, n, tau + half * 512:tau + half * 512 + 512],
                                                 start=(tau == 0), stop=(tau == 3)),
                              reads=[B_dg[s], B_xp[n]], writes=[PB[bank]])
                    if half == 0:
                        act.op(lambda e: e.activation(out=xcb[:, n, half * 512:(half + 1) * 512], in_=PS[bank][:, :], func=AF.Identity,
                                                      bias=chs[:, 0, n:n + 1], scale=1.0), reads=[PB[bank], B_cc], parts=[B_xc[n]])
                    else:
                        dve.op(lambda e: e.tensor_scalar(out=xcb[:, n, half * 512:(half + 1) * 512], in0=PS[bank][:, :],
                                                         scalar1=chs[:, 0, n:n + 1], scalar2=None, op0=ALU.add),
                               reads=[PB[bank], B_cc], parts=[B_xc[n]])
            B_gw = [Buf(), Buf()]
            ds_gw = [K.dsem("gw0"), K.dsem("gw1")]
            B_ft = [[Buf() for _ in range(4)] for _ in range(3)]
            B_stt = Buf()

            def gw_load(n):
                s = n % 2
                pool.dma(ds_gw[s], gw[:, s, 0, :, :], ga_d[n], writes=[B_gw[s]])
                pool.dma(ds_gw[s], gw[:, s, 1, :, :], gx_d[n], parts=[B_gw[s]])
            gw_load(0)
            gw_load(1)
            for n in range(NCH):
                s = n % 2
                h_lo = (128 * n) // 168
                h_hi = (128 * n + 127) // 168
                m_lo = (168 * h_lo) // 128
                m_hi = (168 * (h_hi + 1) - 1) // 128
                sF = n % 3
                Fv = wball[:, sF * KC * 512:(sF + 1) * KC * 512].bitcast(F32)
                F = [Fv[:, k_ * T:(k_ + 1) * T] for k_ in range(4)]
                BF_ = B_ft[sF]
                for gi in range(2):
                    for half in range(2):
                        bank = gi * 2 + half + 4 * s
                        for mi, m in enumerate(range(m_lo, m_hi + 1)):
                            pe.op(lambda e: e.matmul(out=PS[bank][:, :], lhsT=gw[:, s, gi, mi, :],
                                                     rhs=xcb[:, m, half * 512:(half + 1) * 512],
                                                     start=(mi == 0), stop=(m == m_hi)),
                                  reads=[B_gw[s], B_xc[m]], writes=[PB[bank]])
                        act.op(lambda e: e.activation(out=F[gi][:, half * 512:(half + 1) * 512], in_=PS[bank][:, :],
                                                      func=AF.Tanh, bias=chs[:, 7 + gi, n:n + 1], scale=0.5),
                               reads=[PB[bank], B_cc], writes=[BF_[gi]] if half == 0 else (), parts=() if half == 0 else [BF_[gi]])
                if n + 2 < NCH:
                    gw_load(n + 2)
                act.op(lambda e: e.activation(out=F[2], in_=F[0], func=AF.Exp, scale=chs[:, 6, n:n + 1], bias=chs[:, 6, n:n + 1]),
                       reads=[BF_[0], B_cc], writes=[BF_[2]])
                act.op(lambda e: e.activation(out=F[3], in_=F[2], func=AF.Square), reads=[BF_[2]], writes=[BF_[3]])
                act.op(lambda e: e.activation(out=F[3], in_=F[3], func=AF.Sqrt, scale=-0.25, bias=0.25),
                       reads=[BF_[3]], writes=[BF_[3]])
                dve.op(lambda e: e.scalar_tensor_tensor(out=F[1], in0=F[1], scalar=1.0, in1=F[3], op0=ALU.add, op1=ALU.mult),
                       reads=[BF_[3], BF_[1]], writes=[BF_[1]])
                dve.op(lambda e: e.tensor_tensor(out=F[1], in0=F[1], in1=xcb[:, n, :], op=ALU.mult),
                       reads=[BF_[1], B_xc[n]], writes=[BF_[1]])
                dve.op(lambda e: e.tensor_tensor_scan(out=F[3], data0=F[2], data1=F[1], initial=0.0,
                                                      op0=ALU.mult, op1=ALU.add), reads=[BF_[2], BF_[1]], writes=[BF_[3]])
                dve.op(lambda e: e.tensor_tensor_scan(out=F[0], data0=F[2], data1=cst[:, 0, :], initial=1.0,
                                                      op0=ALU.mult, op1=ALU.add), reads=[BF_[2], B_cc], writes=[BF_[0]])
                dve.op(lambda e: e.tensor_copy(out=stt[:, 0, n:n + 1], in_=F[3][:, T - 1:T]),
                       reads=[BF_[3]], parts=[B_stt])
                dve.op(lambda e: e.tensor_copy(out=stt[:, 1, n:n + 1], in_=F[0][:, T - 1:T]),
                       reads=[BF_[0]], parts=[B_stt])
                pool.op(lambda e: e.tensor_tensor(out=xpre[:, n, 0:T], in0=F[0], in1=sz[:, n, :], op=ALU.mult),
                        reads=[BF_[0], B_sz[n]], writes=[B_xp[n]])
                pool.op(lambda e: e.tensor_tensor(out=sz[:, n, :], in0=F[3], in1=sz[:, n, :], op=ALU.mult),
                        reads=[BF_[3], B_sz[n]], writes=[B_sz[n]])
            WC = [wball[:, i * NCH * 512:(i + 1) * NCH * 512].rearrange("p (a b) -> p a b", a=NCH) for i in range(2)]
            B_wc = [Buf(), Buf()]
            ds_wc = [K.dsem("wc0"), K.dsem("wc1")]
            all_ft = [b_ for row in B_ft for b_ in row]

            def wc_load(cg, slot):
                pool.dma(ds_wc[slot], WC[slot][:, :, :],
                         cw_out.rearrange("(kc p) n -> p kc n", p=128)[:, :, cg * 512:(cg + 1) * 512], writes=[B_wc[slot]] + (all_ft if cg < 2 else []))
            wc_load(0, 0)
            wc_load(1, 1)
            B_sl = Buf()
            B_sg = Buf()
            B_sta = Buf()
            ds_st = K.dsem("st")
            ds_sag = K.dsem("st_ag", step=1)
            sp.dma(ds_st, st_l, stt[:].rearrange("p a b -> p (a b)"), reads=[B_stt], writes=[B_sl])
            pool.all_gather(ds_sag, st_g, st_l, reads=[B_sl], writes=[B_sg])
            sp.dma(ds_st, sta[:], st_g.rearrange("(r p) f -> p r f", p=128), reads=[B_sg], writes=[B_sta])
            dve.op(lambda e: e.memset(hin[:], 0.0), writes=[B_sta])
            for r in range(3):
                dve.op(lambda e: e.tensor_tensor(out=hin[:, 4, :], in0=sta[:, r, NCH:2 * NCH], in1=hin[:, r, :], op=ALU.mult),
                       reads=[B_sta], writes=[B_sta])
                dve.op(lambda e: e.tensor_tensor(out=hin[:, r + 1, :], in0=hin[:, 4, :], in1=sta[:, r, 0:NCH], op=ALU.add),
                       reads=[B_sta], writes=[B_sta])
            dve.op(lambda e: e.tensor_scalar(out=hin[:, 5, :], in0=hin[:, 0, :], scalar1=masks[:, 8:9], scalar2=None, op0=ALU.mult),
                   reads=[B_sta, B_const], writes=[B_sta])
            for r in range(1, 4):
                dve.op(lambda e: e.scalar_tensor_tensor(out=hin[:, 5, :], in0=hin[:, r, :], scalar=masks[:, 8 + r:9 + r],
                                                        in1=hin[:, 5, :], op0=ALU.mult, op1=ALU.add),
                       reads=[B_sta, B_const], writes=[B_sta])
            for n in range(NCH):
                dve.op(lambda e: e.scalar_tensor_tensor(out=sz[:, n, :], in0=xpre[:, n, 0:T], scalar=hin[:, 5, n:n + 1],
                                                        in1=sz[:, n, :], op0=ALU.mult, op1=ALU.add),
                       reads=[B_xp[n], B_sta, B_sz[n]], writes=[B_sz[n]])
            K.barrier()
        xc_ctx.__exit__(None, None, None)
        if dbg and stop == 7:
            o2 = nc.dram_tensor("dbg_hz", [128, NCH * T], BF16, kind="ExternalOutput").ap()
            dsd = K.dsem("dbgl2")
            b = Buf()
            sp.dma(dsd, o2, sz[:].rearrange("p a b -> p (a b)"), reads=B_sz, writes=[b])
            dbg_out["l2"] = b

    B_y = Buf()
    if stop >= 6:
        xp_ctx.__exit__(None, None, None)
    if stop >= 8:
        with nc.sbuf_tensor("x2", [128, NT, D], F32) as x2, nc.sbuf_tensor("gfn", [128, D], F32) as gfn, \
             nc.sbuf_tensor("fjk", [128, D], BF16) as fjk, nc.sbuf_tensor("fst", [128, NT, 4], F32) as fst:
            B_x2 = [Buf() for _ in range(NT)]
            ds_x2 = K.dsem("x2")
            ds_g2 = K.dsem("g2")
            ds_o = [K.dsem("o0"), K.dsem("o1")]
            B_gf = Buf()
            B_fj = Buf()
            B_fs = [Buf() for _ in range(NT)]
            sp.dma(ds_g2, gfn[:], vecs[5, :].partition_broadcast(128), writes=[B_gf])

            for t in range(NT):
                sp.dma(ds_x2, x2[:, t, :], x1_d[t * 128:(t + 1) * 128, :], reads=[B_x1], writes=[B_x2[t]])
            for cg in range(4):
                slot = cg % 2
                for t in range(NT):
                    bank = t % 8
                    for n in range(NCH):
                        pe.op(lambda e: e.matmul(out=PS[bank][:, :], lhsT=sz[:, n, t * 128:(t + 1) * 128],
                                                 rhs=WC[slot][:, n, :], start=(n == 0), stop=(n == NCH - 1)),
                              reads=[B_wc[slot], B_sz[n]], writes=[PB[bank]])
                    dve.op(lambda e: e.tensor_tensor(out=x2[:, t, cg * 512:(cg + 1) * 512], in0=PS[bank][:, :],
                                                     in1=x2[:, t, cg * 512:(cg + 1) * 512], op=ALU.add),
                           reads=[PB[bank], B_x2[t]], writes=[B_x2[t]])
                    if cg == 3:
                        act.op(lambda e: e.activation(out=fjk[:], in_=x2[:, t, :], func=AF.Square, accum_out=fst[:, t, 0:1]),
                               reads=[B_x2[t]], writes=[B_fj, B_fs[t]])
                        act.op(lambda e: e.activation(out=fst[:, t, 1:2], in_=fst[:, t, 0:1], func=AF.Sqrt, scale=1.0 / D, bias=EPS),
                               reads=[B_fs[t]], writes=[B_fs[t]])
                        dve.op(lambda e: e.reciprocal(out=fst[:, t, 2:3], in_=fst[:, t, 1:2]), reads=[B_fs[t]], writes=[B_fs[t]])
                        dve.op(lambda e: e.scalar_tensor_tensor(out=x2[:, t, :], in0=x2[:, t, :], scalar=fst[:, t, 2:3], in1=gfn[:],
                                                                op0=ALU.mult, op1=ALU.mult),
                               reads=[B_x2[t], B_fs[t], B_gf], writes=[B_x2[t]])
                        sp.dma(ds_o[t % 2], y_d[t * 128:(t + 1) * 128, :], x2[:, t, :], reads=[B_x2[t]], parts=[B_y])

                if cg + 2 < 4:
                    wc_load(cg + 2, slot)
    if stop >= 6:
        sz_ctx.__exit__(None, None, None)
    if stop < 8:
        with nc.sbuf_tensor("zz", [128, D], F32) as zz:
            bz = Buf()
            dve.op(lambda e: e.memset(zz[:], 0.0), writes=[bz])
            dsz = K.dsem("zz")
            for t in range(NT):
                sp.dma(dsz, y_d[t * 128:(t + 1) * 128, :], zz[:], reads=[bz], parts=[B_y])
            K.barrier()
    K.barrier()
    return nc


def _band(w):
    out = np.zeros((NCH, 128, 4, 128), np.float32)
    for n in range(NCH):
        h_lo = (128 * n) // 168
        h_hi = (128 * n + 127) // 168
        m_lo = (168 * h_lo) // 128
        for h in range(h_lo, h_hi + 1):
            j0 = max(128 * n, 168 * h)
            j1 = min(128 * n + 128, 168 * h + 168)
            for mi in range(4):
                m = m_lo + mi
                i0 = max(128 * m, 168 * h)
                i1 = min(128 * m + 128, 168 * h + 168)
                if i1 <= i0 or j1 <= j0:
                    continue
                out[n, i0 - 128 * m:i1 - 128 * m, mi, j0 - 128 * n:j1 - 128 * n] = \
                    w[h, i0 - 168 * h:i1 - 168 * h, j0 - 168 * h:j1 - 168 * h]
    return out


def make_in_maps(x, ab_norm, ab_w_in, ab_lambda, ab_head_norm, ab_sgu_ln_g, ab_sgu_ln_b, ab_sgu_w, ab_sgu_b,
                 ab_w_out, c_norm, c_w_in, c_conv_w, c_conv_b, c_gate_a_w, c_gate_a_b, c_gate_x_w, c_gate_x_b,
                 c_lambda, c_w_out, final_norm):
    f = lambda a: np.ascontiguousarray(np.asarray(a, dtype=np.float32))
    x = f(x)
    vecs = f(np.stack([f(ab_norm)[0], f(ab_head_norm)[0], f(ab_sgu_ln_g)[0], f(ab_sgu_ln_b)[0], f(c_norm)[0],
                       f(final_norm)]))
    chl = lambda v: f(f(v).reshape(NCH, 128).T)
    shared = {
        "w_in": f(ab_w_in)[0], "w_out": f(ab_w_out)[0], "cw_in": f(c_w_in)[0], "cw_out": f(c_w_out)[0],
        "vecs": vecs, "lam_p": f(ab_lambda)[0].reshape(512),
        "sgu_wT": f(np.transpose(f(ab_sgu_w)[0], (2, 0, 1))),
        "sgu_b": f(f(ab_sgu_b)[0].T),
        "conv_w": f(np.transpose(f(c_conv_w)[0].reshape(4, NCH, 128), (2, 1, 0))),
        "chv": f(np.stack([chl(f(c_conv_b)[0]), chl(f(c_gate_a_b)[0]), chl(f(c_gate_x_b)[0]), chl(f(c_lambda)[0])], axis=1)),
        "ga": _band(f(c_gate_a_w)[0]), "gx": _band(f(c_gate_x_w)[0]),
        "ident": np.eye(128, dtype=np.float32),
    }
    in_maps = []
    for core in range(8):
        b, r = core // 4, core % 4
        m = np.zeros((128, 16), np.float32)
        for rr in range(3):
            m[:, rr] = 0.0 if rr < r else NEG
        if r >= 1:
            m[:, 4 + r - 1] = 1.0
        m[:, 8 + r] = 1.0
        d = dict(shared)
        d["x"] = f(x[b, r * T:(r + 1) * T, :])
        d["masks"] = m
        in_maps.append(d)
    return in_maps


_CACHE = {}


def kernel(**inputs):
    in_maps = make_in_maps(**inputs)
    if "nc" not in _CACHE:
        _CACHE["nc"] = build_program()
    res = run_bass_kernel_spmd(_CACHE["nc"], in_maps, core_ids=list(range(8)))
    out = np.zeros((2, 4096, D), np.float32)
    for core in range(8):
        b, r = core // 4, core % 4
        out[b, r * T:(r + 1) * T, :] = res.results[core]["y"]
    return out
```

```python
import numpy as np
import concourse.bass as bass
import concourse.mybir as mybir
from concourse.bass_utils import run_bass_kernel_spmd

F32 = mybir.dt.float32
BF16 = mybir.dt.bfloat16
AF = mybir.ActivationFunctionType
ALU = mybir.AluOpType
AX = mybir.AxisListType

T = 1024
NT = 8
D = 2048
KC = 16
AB_IN = 14336
CW = 2688
NCH = 21
EPS = 1e-6
LAM_INIT = 0.2
NEG = -30000.0
RG = [[0, 1, 2, 3], [4, 5, 6, 7]]


class Tok:
    __slots__ = ("src", "val")

    def __init__(self, src, val):
        self.src = src
        self.val = val


class Buf:
    __slots__ = ("name", "writer", "extra", "readers")

    def __init__(self, name=""):
        self.name = name
        self.writer = None
        self.extra = {}
        self.readers = []


class DSem:
    def __init__(self, K, name, step=16):
        self.sem = K.nc.alloc_semaphore(name)
        self.count = 0
        self.is_dma = True
        self.step = step
        K.dsems.append(self)


class Eng:
    def __init__(self, K, name, eng):
        self.K = K
        self.name = name
        self.eng = eng
        self.sem = K.nc.alloc_semaphore("prog_" + name)
        self.count = 0
        self.last_ins = None
        self.last_signaled = True
        self.waited = {}
        self.is_dma = False
        self.nwaits = 0

    def _ensure_signaled(self, val):
        if val > self.count:
            assert val == self.count + 1 and self.last_ins is not None and not self.last_signaled, \
                (self.name, val, self.count)
            self.last_ins.then_inc(self.sem, 1)
            self.count += 1
            self.last_signaled = True

    def wait(self, tok, raw=True):
        if tok is None:
            return
        src = tok.src
        if src is self:
            if (not raw) or self.name in ("pe", "sp"):
                return
        if src.is_dma:
            val = src.count
        else:
            src._ensure_signaled(tok.val)
            val = tok.val
        if self.waited.get(id(src), 0) >= val:
            return
        self.eng.wait_ge(src.sem, val)
        self.nwaits += 1
        self.waited[id(src)] = val

    def deps(self, reads, writes, parts):
        for b in reads:
            self.wait(b.writer)
            for w in list(b.extra.values()):
                self.wait(w)
        for b in writes:
            self.wait(b.writer, raw=False)
            for w in list(b.extra.values()):
                self.wait(w, raw=False)
            for r in b.readers:
                self.wait(r, raw=False)
        for b in parts:
            for r in b.readers:
                self.wait(r, raw=False)

    def _record(self, tok, reads, writes, parts):
        for b in reads:
            b.readers.append(tok)
        for b in writes:
            b.writer = tok
            b.extra = {}
            b.readers = []
        for b in parts:
            b.extra[id(tok.src)] = tok

    def op(self, fn, reads=(), writes=(), parts=(), signal=False):
        self.deps(reads, writes, parts)
        ins = fn(self.eng)
        self.last_ins = ins
        self.last_signaled = False
        tok = Tok(self, self.count + 1)
        if signal or self.name != "pe":
            self._ensure_signaled(self.count + 1)
        self._record(tok, reads, writes, parts)
        return ins

    def dma(self, dsem, out, in_, reads=(), writes=(), parts=(), **kw):
        self.deps(reads, writes, parts)
        ins = self.eng.dma_start(out=out, in_=in_, **kw)
        ins.then_inc(dsem.sem, 16)
        dsem.count += 16
        self._record(Tok(dsem, dsem.count), reads, writes, parts)
        return ins

    def all_gather(self, dsem, out, in_, reads=(), writes=(), qos=None):
        self.deps(reads, writes, ())
        if qos is None:
            ins = self.eng.collective_compute("AllGather", ALU.bypass, replica_groups=RG, ins=[in_], outs=[out])
        else:
            ins = self.eng.collective_compute("AllGather", ALU.bypass, replica_groups=RG, ins=[in_], outs=[out], dma_qos=qos)
        ins.then_inc(dsem.sem, 1)
        dsem.count += 1
        self._record(Tok(dsem, dsem.count), reads, writes, ())
        return ins


class Kern:
    def __init__(self, nc):
        self.nc = nc
        self.dsems = []
        self.pe = Eng(self, "pe", nc.tensor)
        self.act = Eng(self, "act", nc.scalar)
        self.dve = Eng(self, "dve", nc.vector)
        self.pool = Eng(self, "pool", nc.gpsimd)
        self.sp = Eng(self, "sp", nc.sync)
        self.engs = [self.pe, self.act, self.dve, self.pool, self.sp]

    def dsem(self, name, step=16):
        return DSem(self, name, step)

    def barrier(self):
        sp = self.sp
        for e in self.engs:
            if e.last_ins is not None and not e.last_signaled:
                e._ensure_signaled(e.count + 1)
        for s in self.engs:
            if s is not sp and s.count > 0:
                sp.wait(Tok(s, s.count))
        for d in self.dsems:
            if d.count > 0 and d.step == 16:
                sp.wait(Tok(d, d.count))
        sp.eng.sem_inc(sp.sem, 1)
        sp.count += 1
        sp.last_signaled = True
        for e in self.engs:
            if e is not sp:
                e.wait(Tok(sp, sp.count))


def build_program(stop=99, dbg=False):
    nc = bass.Bass("TRN2", target_bir_lowering=False)
    K = Kern(nc)
    pe, act, dve, pool, sp = K.pe, K.act, K.dve, K.pool, K.sp

    def din(name, shape, dt=F32):
        return nc.dram_tensor(name, shape, dt, kind="ExternalInput").ap()

    def dscr(name, shape, dt):
        return nc.dram_tensor(name, shape, dt).ap()

    x_d = din("x", [T, D])
    w_in = din("w_in", [D, AB_IN])
    w_out = din("w_out", [4096, D])
    cw_in = din("cw_in", [D, 2 * CW])
    cw_out = din("cw_out", [CW, D])
    vecs = din("vecs", [6, D])
    lam_p = din("lam_p", [512])
    sgu_wT = din("sgu_wT", [128, 8, 128])
    sgu_b = din("sgu_b", [128, 8])
    conv_w = din("conv_w", [128, NCH, 4])
    chv = din("chv", [128, 4, NCH])
    ga_d = din("ga", [NCH, 128, 4, 128])
    gx_d = din("gx", [NCH, 128, 4, 128])
    ident_d = din("ident", [128, 128])
    masks_d = din("masks", [128, 16])
    y_d = nc.dram_tensor("y", [T, D], F32, kind="ExternalOutput").ap()

    qT_d = dscr("qT_d", [128, 16 * T], BF16)
    kvl = [dscr("kvl%d" % h, [256, 2048], BF16) for h in range(8)]
    kvg = [dscr("kvg%d" % h, [1024, 2048], BF16) for h in range(8)]
    sza_d = dscr("sza_d", [T, D], BF16)
    ymix_d = dscr("ymix_d", [T, 4096], BF16)
    x1_d = dscr("x1_d", [T, D], F32)
    halo_l = dscr("halo_l", [128, NCH * 3], BF16)
    halo_g = dscr("halo_g", [512, NCH * 3], BF16)
    st_l = dscr("st_l", [128, 2 * NCH], F32)
    st_g = dscr("st_g", [512, 2 * NCH], F32)

    dbg_out = {}

    def dbg_dump(name, src_ap, shape, dt, reads):
        o = nc.dram_tensor("dbg_" + name, shape, dt, kind="ExternalOutput").ap()
        ds = K.dsem("dbg_" + name)
        b = Buf()
        sp.dma(ds, o, src_ap, reads=reads, writes=[b])
        dbg_out[name] = b

    B_qT = [Buf() for _ in range(16)]
    B_kvl = [Buf() for _ in range(8)]
    B_kvg = [Buf() for _ in range(8)]
    B_sza = Buf()
    B_ymix = Buf()
    B_x1 = Buf()

    PS = [nc.alloc_psum_tensor("ps%d" % i, [128, 512], F32) for i in range(8)]
    PB = [Buf("ps%d" % i) for i in range(8)]

    ident = nc.alloc_sbuf_tensor("ident_sb", [128, 128], BF16)
    masks = nc.alloc_sbuf_tensor("masks_sb", [128, 16], F32)
    ones_b = nc.alloc_sbuf_tensor("ones_b", [128, 2], BF16)
    B_const = Buf("const")
    ds_c = K.dsem("const")
    pool.dma(ds_c, ident[:], ident_d, parts=[B_const])
    sp.dma(ds_c, masks[:], masks_d, parts=[B_const])
    dve.op(lambda e: e.memset(ones_b[:], 1.0), parts=[B_const])

    rr = [0]

    def evac(out, in_, reads, writes=(), scale=None, eng=None, parts=()):
        if eng is None:
            eng = act if (rr[0] % 2 == 0) else dve
            rr[0] += 1
        if eng is act:
            if scale is None:
                act.op(lambda e: e.activation(out=out, in_=in_, func=AF.Copy), reads=reads, writes=writes, parts=parts)
            else:
                act.op(lambda e: e.activation(out=out, in_=in_, func=AF.Copy, scale=float(scale)),
                       reads=reads, writes=writes, parts=parts)
        else:
            if scale is None:
                dve.op(lambda e: e.tensor_copy(out=out, in_=in_), reads=reads, writes=writes, parts=parts)
            else:
                dve.op(lambda e: e.tensor_scalar(out=out, in0=in_, scalar1=float(scale), scalar2=None,
                                                 op0=ALU.mult), reads=reads, writes=writes, parts=parts)

    wball = nc.alloc_sbuf_tensor("wball", [128, 3 * KC * 512], BF16)
    WB = [wball[:, i * KC * 512:(i + 1) * KC * 512].rearrange("p (a b) -> p a b", a=KC) for i in range(3)]
    B_wb = [Buf("wb%d" % i) for i in range(3)]
    ds_wb = [K.dsem("wb%d" % i) for i in range(3)]

    wslot_ctr = [0]

    def run_stream(jobs, nslots=3):
        wj = [j for j in jobs if j[0] is not None]
        base = wslot_ctr[0]
        wslot_ctr[0] += len(wj)
        slot_of = {id(j): (base + k) % nslots for k, j in enumerate(wj)}
        for k in range(min(nslots, len(wj))):
            wj[k][0]((base + k) % nslots)
        nxt = min(nslots, len(wj))
        for job in jobs:
            if job[0] is None:
                job[1](None)
                continue
            job[1](slot_of[id(job)])
            if nxt < len(wj):
                wj[nxt][0]((base + nxt) % nslots)
                nxt += 1

    def wload(src2d, col0, ncols):
        def f(slot):
            pool.dma(ds_wb[slot], WB[slot][:, :, 0:ncols],
                     src2d.rearrange("(kc p) n -> p kc n", p=128)[:, :, col0:col0 + ncols],
                     writes=[B_wb[slot]])
        return f

    def norm_phase(src_d, B_src, gvec_row, xnT, B_xn, tg):
        with nc.sbuf_tensor("gbc" + tg, [128, D], F32) as gbc, \
             nc.sbuf_tensor("xt0" + tg, [128, D], F32) as xt0, nc.sbuf_tensor("xt1" + tg, [128, D], F32) as xt1, \
             nc.sbuf_tensor("junk" + tg, [128, D], BF16) as junk, \
             nc.sbuf_tensor("xn0" + tg, [128, D], BF16) as xn0, nc.sbuf_tensor("xn1" + tg, [128, D], BF16) as xn1, \
             nc.sbuf_tensor("nst" + tg, [128, 4 * NT], F32) as nst:
            xts = [xt0, xt1]
            xns = [xn0, xn1]
            B_xt = [Buf(), Buf()]
            B_xnk = [Buf(), Buf()]
            B_g = Buf()
            B_junk = Buf()
            B_st = [Buf() for _ in range(NT)]
            ds_x = [K.dsem("nx0" + tg), K.dsem("nx1" + tg)]
            ds_g = K.dsem("ng" + tg)
            sp.dma(ds_g, gbc[:], vecs[gvec_row, :].partition_broadcast(128), writes=[B_g])
            for t in range(min(2, NT)):
                sp.dma(ds_x[t % 2], xts[t % 2][:], src_d[t * 128:(t + 1) * 128, :], reads=[B_src], writes=[B_xt[t % 2]])
            def stage_a(t):
                s = t % 2
                xt = xts[s]
                act.op(lambda e: e.activation(out=junk[:], in_=xt[:], func=AF.Square,
                                              accum_out=nst[:, 4 * t:4 * t + 1]),
                       reads=[B_xt[s]], writes=[B_junk, B_st[t]])
                act.op(lambda e: e.activation(out=nst[:, 4 * t + 1:4 * t + 2], in_=nst[:, 4 * t:4 * t + 1],
                                              func=AF.Sqrt, scale=1.0 / D, bias=EPS),
                       reads=[B_st[t]], writes=[B_st[t]])
                dve.op(lambda e: e.reciprocal(out=nst[:, 4 * t + 2:4 * t + 3], in_=nst[:, 4 * t + 1:4 * t + 2]),
                       reads=[B_st[t]], writes=[B_st[t]])
                xn = xns[s]
                dve.op(lambda e: e.scalar_tensor_tensor(out=xn[:], in0=xt[:], scalar=nst[:, 4 * t + 2:4 * t + 3],
                                                        in1=gbc[:], op0=ALU.mult, op1=ALU.mult),
                       reads=[B_xt[s], B_st[t], B_g], writes=[B_xnk[s]])
                if t + 2 < NT:
                    sp.dma(ds_x[s], xts[s][:], src_d[(t + 2) * 128:(t + 3) * 128, :], reads=[B_src],
                           writes=[B_xt[s]])

            def stage_b(t):
                s = t % 2
                xn = xns[s]
                for q4 in range(4):
                    bank = (t * 4 + q4) % 8
                    pv = PS[bank][:].bitcast(BF16)
                    for j in range(4):
                        c = q4 * 4 + j
                        pe.op(lambda e: e.transpose(out=pv[:, j * 128:(j + 1) * 128],
                                                    in_=xn[:, c * 128:(c + 1) * 128], identity=ident[:]),
                              reads=[B_xnk[s], B_const], writes=[PB[bank]])
                    evac(xnT[:, q4 * 4:q4 * 4 + 4, t * 128:(t + 1) * 128],
                         pv[:, 0:512].rearrange("p (a b) -> p a b", a=4),
                         reads=[PB[bank]], parts=[B_xn[t]])

            stage_a(0)
            for t in range(NT):
                if t + 1 < NT:
                    stage_a(t + 1)
                stage_b(t)

    L0 = nc.sbuf_tensor("xnT", [128, KC, T], BF16)
    xnT = L0.__enter__()
    B_xn = [Buf("xnT%d" % t) for t in range(NT)]
    st_ctx = nc.sbuf_tensor("stg", [128, 4, T], BF16)
    stg = st_ctx.__enter__()
    if stop >= 3:
        vln_ctx = nc.sbuf_tensor("vln", [128, NT, D], BF16)
        vln = vln_ctx.__enter__()
        B_vln = [Buf() for _ in range(NT)]
        cst_ctx = nc.sbuf_tensor("sgc", [128, 3, D], F32)
        sgc = cst_ctx.__enter__()
        wsT_ctx = nc.sbuf_tensor("wsT", [128, 8, 128], BF16)
        wsT = wsT_ctx.__enter__()
        sb_ctx = nc.sbuf_tensor("sgb", [128, 16], F32)
        sgb = sb_ctx.__enter__()
    norm_phase(x_d, Buf(), 0, xnT, B_xn, "_a")
    if dbg and stop == 1:
        o = nc.dram_tensor("dbg_xnT", [128, KC * T], BF16, kind="ExternalOutput").ap()
        dsd = K.dsem("dbg0")
        b0 = Buf()
        sp.dma(dsd, o, xnT[:].rearrange("p a b -> p (a b)"), reads=B_xn, writes=[b0])
        dbg_out["xnT"] = b0

    if stop >= 2:
        B_stg = [Buf() for _ in range(4)]
        ds_stg = [K.dsem("stg%d" % i) for i in range(4)]
        sti = [0]

        def qk_job(kind, g):
            col0 = (0 if kind == "q" else 2048) + g * 512

            def comp(slot):
                for j in range(4):
                    hm = g * 4 + j
                    si = sti[0] % 4
                    sti[0] += 1
                    for half in range(2):
                        bank = (hm * 2 + half) % 8
                        for c in range(KC):
                            pe.op(lambda e: e.matmul(out=PS[bank][:, :], lhsT=WB[slot][:, c, j * 128:(j + 1) * 128],
                                                     rhs=xnT[:, c, half * 512:(half + 1) * 512],
                                                     start=(c == 0), stop=(c == KC - 1)),
                                  reads=[B_wb[slot]] + B_xn[half * 4:half * 4 + 4], writes=[PB[bank]])
                        evac(stg[:, si, half * 512:(half + 1) * 512], PS[bank][:, :], reads=[PB[bank]],
                             writes=[B_stg[si]], scale=(128.0 ** -0.5 if kind == "q" else None))
                    if kind == "q":
                        sp.dma(ds_stg[si], qT_d[:, hm * T:(hm + 1) * T], stg[:, si, :], reads=[B_stg[si]],
                               writes=[B_qT[hm]])
                    else:
                        sp.dma(ds_stg[si], kvl[hm // 2][0:128, (hm % 2) * T:(hm % 2 + 1) * T], stg[:, si, :], reads=[B_stg[si]],
                               parts=[B_kvl[hm // 2]])
            return (wload(w_in, col0, 512), comp)

        def tok_job(col0, sink):
            def comp(slot):
                for t in range(NT):
                    bank = t % 8
                    for c in range(KC):
                        pe.op(lambda e: e.matmul(out=PS[bank][:, :], lhsT=xnT[:, c, t * 128:(t + 1) * 128],
                                                 rhs=WB[slot][:, c, 0:512], start=(c == 0), stop=(c == KC - 1)),
                              reads=[B_wb[slot], B_xn[t]], writes=[PB[bank]])
                    sink(t, bank)
            return (wload(w_in, col0, 512), comp)

        def v_sink(g):
            def f(t, bank):
                si = sti[0] % 4
                sti[0] += 1
                evac(stg[:, si, 0:512], PS[bank][:, :], reads=[PB[bank]], writes=[B_stg[si]])
                for hh in range(2):
                    sp.dma(ds_stg[si], kvl[2 * g + hh][128:256, t * 256:(t + 1) * 256], stg[:, si, hh * 256:(hh + 1) * 256],
                           reads=[B_stg[si]], parts=[B_kvl[2 * g + hh]])
            return f

        def za_sink(g):
            def f(t, bank):
                si = sti[0] % 4
                sti[0] += 1
                act.op(lambda e: e.activation(out=stg[:, si, 0:512], in_=PS[bank][:, :], func=AF.Silu),
                       reads=[PB[bank]], writes=[B_stg[si]])
                sp.dma(ds_stg[si], sza_d[t * 128:(t + 1) * 128, g * 512:(g + 1) * 512], stg[:, si, 0:512],
                       reads=[B_stg[si]], parts=[B_sza])
            return f

        def ag_job(g, heads=None):
            def comp(slot):
                for h in (heads if heads is not None else (2 * g, 2 * g + 1)):
                    ds_ag = K.dsem("ag_kv%d" % h, step=1)
                    pool.all_gather(ds_ag, kvg[h], kvl[h], reads=[B_kvl[h]], writes=[B_kvg[h]], qos="P2")
            return (None, comp)
        kq = [[qk_job("k", g), tok_job(4096 + g * 512, v_sink(g)), qk_job("q", g)] for g in range(4)]
        if stop < 3:
            jobs = []
            for g in range(4):
                jobs += kq[g] + [ag_job(g)]
            run_stream(jobs)
        if dbg and stop == 2:
            dbg_dump("qT", qT_d, [128, 16 * T], BF16, B_qT)
            dbg_dump("kvg0", kvg[0], [1024, 2048], BF16, [B_kvg[0]])
            dbg_dump("kvg7", kvg[7], [1024, 2048], BF16, [B_kvg[7]])

    if stop >= 3:
        B_sgc = Buf()
        B_ws = Buf()
        ds_s = K.dsem("sgc")
        ds_ws = K.dsem("wsT")
        pool.dma(ds_ws, wsT[:], sgu_wT, writes=[B_ws])
        sp.dma(ds_s, sgc[:, 0, :], vecs[2, :].partition_broadcast(128), parts=[B_sgc])
        sp.dma(ds_s, sgc[:, 1, :], vecs[3, :].partition_broadcast(128), parts=[B_sgc])
        sp.dma(ds_s, sgb[:, 0:8], sgu_b, parts=[B_sgc])
        dve.op(lambda e: e.memset(wsT[64:128, :, 0:64], 0.0), reads=[B_ws], writes=[B_ws])
        for g in range(8):
            pe.op(lambda e: e.matmul(out=PS[0][:, g:g + 1], lhsT=wsT[:, g, :], rhs=ones_b[:, 0:1],
                                     start=True, stop=True), reads=[B_ws, B_const], writes=[PB[0]])
        dve.op(lambda e: e.tensor_copy(out=sgb[:, 8:16], in_=PS[0][:, 0:8]), reads=[PB[0], B_sgc], writes=[B_sgc])
        for g in range(8):
            dve.op(lambda e: e.tensor_scalar(out=sgc[:, 2, g * 256:(g + 1) * 256], in0=sgc[:, 1, g * 256:(g + 1) * 256],
                                             scalar1=sgb[:, 8 + g:9 + g], scalar2=sgb[:, g:g + 1],
                                             op0=ALU.mult, op1=ALU.add), reads=[B_sgc], writes=[B_sgc])
        za = [tok_job(6144 + g * 512, za_sink(g)) for g in range(4)]
        jobs = []
        for g in range(4):
            jobs += kq[g]
        jobs += [za[0], ag_job(0, heads=(0,)), za[1], ag_job(0, heads=(1,)), za[2], za[3]]
        run_stream(jobs)
        with nc.sbuf_tensor("bst", [128, NT, 4, 6], F32) as bst, nc.sbuf_tensor("mv", [128, NT, 4], F32) as mv, \
             nc.sbuf_tensor("ug", [128, NT, 512], F32) as ug, nc.sbuf_tensor("tmpb", [128, 4, 512], F32) as tmpb:
            B_bst = [Buf() for _ in range(NT)]

            def vb_job(g):
                def comp(slot):
                    for t in range(NT):
                        bank = (g * NT + t) % 8
                        for c in range(KC):
                            pe.op(lambda e: e.matmul(out=PS[bank][:, :], lhsT=xnT[:, c, t * 128:(t + 1) * 128],
                                                     rhs=WB[slot][:, c, 0:512], start=(c == 0), stop=(c == KC - 1)),
                                  reads=[B_wb[slot], B_xn[t]], writes=[PB[bank]])
                        act.op(lambda e: e.activation(out=vln[:, t, g * 512:(g + 1) * 512], in_=PS[bank][:, :],
                                                      func=AF.Gelu_apprx_tanh), reads=[PB[bank]], parts=[B_vln[t]])
                        dve.op(lambda e: e.bn_stats(out=bst[:, t, g, :], in_=vln[:, t, g * 512:(g + 1) * 512]),
                               reads=[B_vln[t]], parts=[B_bst[t]])
                return (wload(w_in, 10240 + g * 512, 512), comp)
            run_stream([vb_job(g) for g in range(4)])
            for t in range(NT):
                dve.op(lambda e: e.bn_aggr(out=mv[:, t, 0:2], in_=bst[:, t, :, :].rearrange("p a b -> p (a b)")),
                       reads=[B_bst[t]], writes=[B_bst[t]])
                act.op(lambda e: e.activation(out=mv[:, t, 2:3], in_=mv[:, t, 1:2], func=AF.Sqrt, bias=EPS, scale=1.0),
                       reads=[B_bst[t]], writes=[B_bst[t]])
                dve.op(lambda e: e.reciprocal(out=mv[:, t, 3:4], in_=mv[:, t, 2:3]), reads=[B_bst[t]], writes=[B_bst[t]])
                dve.op(lambda e: e.tensor_scalar(out=vln[:, t, :], in0=vln[:, t, :], scalar1=mv[:, t, 0:1],
                                                 scalar2=mv[:, t, 3:4], op0=ALU.subtract, op1=ALU.mult),
                       reads=[B_vln[t], B_bst[t]], writes=[B_vln[t]])
            B_ug = [Buf() for _ in range(NT)]
            B_tm = [Buf() for _ in range(4)]
            jobs = []
            for cg in range(4):
                def u_sink(t, bank):
                    act.op(lambda e: e.activation(out=ug[:, t, :], in_=PS[bank][:, :], func=AF.Gelu_apprx_tanh),
                           reads=[PB[bank]], writes=[B_ug[t]])

                def z_sink(t, bank, cg=cg):
                    mb = (bank + 4) % 8
                    for gg in range(2):
                        g = cg * 2 + gg
                        pe.op(lambda e: e.matmul(out=PS[mb][:, gg * 256:(gg + 1) * 256], lhsT=wsT[:, g, :],
                                                 rhs=vln[:, t, g * 256:(g + 1) * 256], start=True, stop=True),
                              reads=[B_ws, B_vln[t]], writes=[PB[mb]])
                    cs = slice(cg * 512, (cg + 1) * 512)
                    act.op(lambda e: e.activation(out=tmpb[:, 0, :], in_=PS[bank][:, :], func=AF.Silu),
                           reads=[PB[bank]], writes=[B_tm[0]])
                    dve.op(lambda e: e.tensor_tensor(out=tmpb[:, 1, :], in0=PS[mb][:, :], in1=sgc[:, 0, cs], op=ALU.mult),
                           reads=[PB[mb], B_sgc], writes=[B_tm[1]])
                    dve.op(lambda e: e.tensor_tensor(out=tmpb[:, 2, :], in0=tmpb[:, 1, :], in1=sgc[:, 2, cs], op=ALU.add),
                           reads=[B_tm[1], B_sgc], writes=[B_tm[2]])
                    dve.op(lambda e: e.tensor_tensor(out=tmpb[:, 3, :], in0=tmpb[:, 2, :], in1=ug[:, t, :], op=ALU.mult),
                           reads=[B_tm[2], B_ug[t]], writes=[B_tm[3]])
                    si = sti[0] % 4
                    sti[0] += 1
                    dve.op(lambda e: e.tensor_tensor(out=stg[:, si, 0:512], in0=tmpb[:, 3, :], in1=tmpb[:, 0, :], op=ALU.mult),
                           reads=[B_tm[3], B_tm[0]], writes=[B_stg[si]])
                    sp.dma(ds_stg[si], ymix_d[t * 128:(t + 1) * 128, 2048 + cg * 512:2048 + (cg + 1) * 512],
                           stg[:, si, 0:512], reads=[B_stg[si]], parts=[B_ymix])
                jobs.append(tok_job(8192 + cg * 512, u_sink))
                jobs.append(tok_job(12288 + cg * 512, z_sink))
            run_stream(jobs)
            K.barrier()
        sb_ctx.__exit__(None, None, None)
        wsT_ctx.__exit__(None, None, None)
        cst_ctx.__exit__(None, None, None)
        vln_ctx.__exit__(None, None, None)
        if dbg and stop == 3:
            dbg_dump("sza", sza_d, [T, D], BF16, [B_sza])
            dbg_dump("ymix", ymix_d, [T, 4096], BF16, [B_ymix])
    st_ctx.__exit__(None, None, None)
    L0.__exit__(None, None, None)
    K.barrier()

    if stop >= 4:
        NPT = 5
        LOOK = 2
        with nc.sbuf_tensor("qh", [128, 2, 2, T], BF16) as qh, nc.sbuf_tensor("kh", [128, 2, 4, 2, T], BF16) as kh, \
             nc.sbuf_tensor("vh", [128, 2, 4, NT, 256], BF16) as vh, nc.sbuf_tensor("pT", [128, NPT, 2, 256], BF16) as pT, \
             nc.sbuf_tensor("ghn", [128, D], F32) as ghn, nc.sbuf_tensor("lp", [128, 4, 128], F32) as lp, \
             nc.sbuf_tensor("lsc", [128, 16], F32) as lsc, nc.sbuf_tensor("ljk", [128, 128], F32) as ljk, \
             nc.sbuf_tensor("ep", [128, 2, 4, 256], F32) as ep, nc.sbuf_tensor("es", [128, 2, 16], F32) as es, \
             nc.sbuf_tensor("szt", [128, 4, 256], BF16) as szt, nc.sbuf_tensor("yst", [128, 2, 256], BF16) as yst, \
             nc.sbuf_tensor("ejk", [128, 256], BF16) as ejk:
            B_c4 = Buf()
            ds4 = K.dsem("c4")
            sp.dma(ds4, ghn[:], vecs[1, :].partition_broadcast(128), writes=[B_c4])
            sp.dma(ds4, lp[:].rearrange("p a b -> p (a b)"), lam_p.partition_broadcast(128), parts=[B_c4])
            dve.op(lambda e: e.tensor_scalar(out=ghn[:], in0=ghn[:], scalar1=1.0 - LAM_INIT, scalar2=None, op0=ALU.mult),
                   reads=[B_c4], writes=[B_c4])
            for i in range(2):
                dve.op(lambda e: e.tensor_tensor(out=ljk[:], in0=lp[:, 2 * i, :], in1=lp[:, 2 * i + 1, :], op=ALU.mult),
                       reads=[B_c4], writes=[B_c4])
                dve.op(lambda e: e.reduce_sum(out=lsc[:, i:i + 1], in_=ljk[:], axis=AX.X), reads=[B_c4], writes=[B_c4])
            act.op(lambda e: e.activation(out=lsc[:, 2:4], in_=lsc[:, 0:2], func=AF.Exp), reads=[B_c4], writes=[B_c4])
            dve.op(lambda e: e.scalar_tensor_tensor(out=lsc[:, 4:5], in0=lsc[:, 3:4], scalar=-LAM_INIT, in1=lsc[:, 2:3],
                                                    op0=ALU.add, op1=ALU.subtract), reads=[B_c4], writes=[B_c4])
            dve.op(lambda e: e.memset(lsc[:, 5:6], -0.5), reads=[B_c4], writes=[B_c4])
            B_qh = [Buf(), Buf()]
            B_kh = [[Buf() for _ in range(4)] for _ in range(2)]
            B_vh = [[Buf() for _ in range(4)] for _ in range(2)]
            B_pT = [Buf() for _ in range(NPT)]
            ds_q = [K.dsem("q0"), K.dsem("q1")]
            ds_k = [[K.dsem("k%d%d" % (a_, b_)) for b_ in range(4)] for a_ in range(2)]
            ds_v = [[K.dsem("v%d%d" % (a_, b_)) for b_ in range(4)] for a_ in range(2)]
            ds_sz = [K.dsem("sz%d" % i_) for i_ in range(4)]
            ds_y = [K.dsem("y0"), K.dsem("y1")]
            B_ep = [Buf(), Buf()]
            B_es = [Buf(), Buf()]
            B_szt = [Buf() for _ in range(4)]

            def load_szt(h, qb):
                for qs in range(2):
                    z_ = ((h * 4 + qb) % 2) * 2 + qs
                    t = qb * 2 + qs
                    sp.dma(ds_sz[z_], szt[:, z_, :], sza_d[t * 128:(t + 1) * 128, h * 256:(h + 1) * 256],
                           reads=[B_sza], writes=[B_szt[z_]])
            B_yst = [Buf(), Buf()]
            B_ejk = Buf()
            pti = [0]
            sbi = [0]
            epi = [0]
            SB = [5, 6, 7]
            OBK = [[0, 1], [3, 4]]
            SUMB = 2
            B_sum = Buf()

            def load_head(h):
                hp = h % 2
                srcs = [(kvl[h], 0, B_kvl[h])] + [(kvg[h], 256 * r, B_kvg[h]) for r in range(3)]
                sp.dma(ds_q[hp], qh[:, hp, :, :].rearrange("p m t -> p (m t)"), qT_d[:, h * 2 * T:(h + 1) * 2 * T],
                       reads=B_qT[2 * h:2 * h + 2], writes=[B_qh[hp]])
                for si_, (src, row0, B_src) in enumerate(srcs):
                    sp.dma(ds_k[hp][si_], kh[:, hp, si_, :, :].rearrange("p m t -> p (m t)"),
                           src[row0:row0 + 128, :], reads=[B_src], writes=[B_kh[hp][si_]])
                    sp.dma(ds_v[hp][si_], vh[:, hp, si_, :, :].rearrange("p t c -> p (t c)"),
                           src[row0 + 128:row0 + 256, :], reads=[B_src], writes=[B_vh[hp][si_]])

            allv = []
            for h in range(8):
                for qb in range(4):
                    vis = []
                    for kt in range(2 * qb + 2):
                        qsd = kt - 2 * qb
                        vis.append((0, kt, max(qsd, 0) * 128, qsd >= 0, None))
                    for r in range(3):
                        for kt in range(NT):
                            vis.append((1 + r, kt, 0, False, r))
                    for i_, v_ in enumerate(vis):
                        allv.append((h, qb, v_, i_ == 0, i_ == len(vis) - 1))

            def emit_scores(h, qb, v_):
                (si_, kt, qcol0, diag, r) = v_
                hp = h % 2
                bank = SB[sbi[0] % len(SB)]
                sbi[0] += 1
                for m in range(2):
                    pe.op(lambda e: e.matmul(out=PS[bank][:, m * 256 + qcol0:(m + 1) * 256],
                                             lhsT=kh[:, hp, si_, m, kt * 128:(kt + 1) * 128],
                                             rhs=qh[:, hp, m, qb * 256 + qcol0:qb * 256 + 256], start=True, stop=True),
                          reads=[B_kh[hp][si_], B_qh[hp]], writes=[PB[bank]])
                ps_ = pti[0] % NPT
                pti[0] += 1
                src_ap = PS[bank][:, :].rearrange("p (m q) -> p m q", m=2)[:, :, qcol0:256]
                if r is None:
                    act.op(lambda e: e.activation(out=pT[:, ps_, :, qcol0:256], in_=src_ap, func=AF.Exp),
                           reads=[PB[bank]], writes=[B_pT[ps_]])
                else:
                    act.op(lambda e: e.activation(out=pT[:, ps_, :, qcol0:256], in_=src_ap, func=AF.Exp,
                                                  bias=masks[:, r:r + 1], scale=1.0),
                           reads=[PB[bank], B_const], writes=[B_pT[ps_]])
                if diag:
                    act.op(lambda e: e.activation(out=pT[64:128, ps_, :, qcol0:qcol0 + 64], in_=pT[64:128, ps_, :, qcol0:qcol0 + 64],
                                                  func=AF.Copy, scale=0.0), reads=[B_pT[ps_]], writes=[B_pT[ps_]])
                return ps_

            def emit_pv(h, qb, v_, ps_, first):
                (si_, kt, qcol0, diag, r) = v_
                hp = h % 2
                par = (h * 4 + qb) % 2
                for qs in range(qcol0 // 128, 2):
                    ob = OBK[par][qs]
                    for m in range(2):
                        pe.op(lambda e: e.matmul(out=PS[ob][:, m * 256:(m + 1) * 256],
                                                 lhsT=pT[:, ps_, m, qs * 128:(qs + 1) * 128], rhs=vh[:, hp, si_, kt, :],
                                                 start=(first and m == 0), stop=False, skip_group_check=True),
                              reads=[B_pT[ps_], B_vh[hp][si_]], writes=[PB[ob]])
                        c_ = par * 4 + qs * 2 + m
                        pe.op(lambda e: e.matmul(out=PS[SUMB][:, c_:c_ + 1],
                                                 lhsT=pT[:, ps_, m, qs * 128:(qs + 1) * 128], rhs=ones_b[:, 0:1],
                                                 start=(first and m == 0 and qs == 0), stop=False, skip_group_check=True),
                              reads=[B_pT[ps_], B_const], writes=[B_sum])

            def emit_epilogue(h, qb):
                par = (h * 4 + qb) % 2
                es_ = []
                for qs in range(2):
                    e_ = epi[0] % 2
                    epi[0] += 1
                    es_.append(e_)
                    dve.op(lambda e: e.reciprocal(out=es[:, e_, 0:2], in_=PS[SUMB][:, par * 4 + qs * 2:par * 4 + qs * 2 + 2]),
                           reads=[B_sum], writes=[B_es[e_]])
                for qs in range(2):
                    ob = OBK[par][qs]
                    e_ = es_[qs]
                    t = qb * 2 + qs
                    z_ = par * 2 + qs
                    dve.op(lambda e: e.tensor_tensor(out=es[:, e_, 2:3], in0=es[:, e_, 1:2], in1=lsc[:, 4:5], op=ALU.mult),
                           reads=[B_es[e_], B_c4], writes=[B_es[e_]])
                    dve.op(lambda e: e.tensor_scalar(out=ep[:, e_, 0, :], in0=PS[ob][:, 0:256], scalar1=es[:, e_, 0:1],
                                                     scalar2=None, op0=ALU.mult), reads=[PB[ob], B_es[e_]], writes=[B_ep[e_]])
                    dve.op(lambda e: e.scalar_tensor_tensor(out=ep[:, e_, 1, :], in0=PS[ob][:, 256:512], scalar=es[:, e_, 2:3],
                                                            in1=ep[:, e_, 0, :], op0=ALU.mult, op1=ALU.add),
                           reads=[PB[ob], B_es[e_], B_ep[e_]], writes=[B_ep[e_]])
                    dve.op(lambda e: e.scalar_tensor_tensor(out=ep[:, e_, 3, :], in0=ep[:, e_, 1, :], scalar=1.0, in1=ep[:, e_, 1, :],
                                                            op0=ALU.mult, op1=ALU.mult, accum_out=es[:, e_, 4:5]),
                           reads=[B_ep[e_]], writes=[B_es[e_]])
                    dve.op(lambda e: e.tensor_scalar(out=es[:, e_, 5:6], in0=es[:, e_, 4:5], scalar1=1.0 / 256, scalar2=EPS,
                                                     op0=ALU.mult, op1=ALU.add), reads=[B_es[e_]], writes=[B_es[e_]])
                    pool.op(lambda e: e.tensor_tensor(out=es[:, e_, 6:7], in0=es[:, e_, 5:6], in1=lsc[:, 5:6], op=ALU.pow),
                            reads=[B_es[e_], B_c4], writes=[B_es[e_]])
                    dve.op(lambda e: e.scalar_tensor_tensor(out=ep[:, e_, 2, :], in0=ep[:, e_, 1, :], scalar=es[:, e_, 6:7],
                                                            in1=ghn[:, h * 256:(h + 1) * 256], op0=ALU.mult, op1=ALU.mult),
                           reads=[B_ep[e_], B_es[e_], B_c4], writes=[B_ep[e_]])
                    dve.op(lambda e: e.tensor_tensor(out=yst[:, e_, :], in0=ep[:, e_, 2, :], in1=szt[:, z_, :], op=ALU.mult),
                           reads=[B_ep[e_], B_szt[z_]], writes=[B_yst[e_]])
                    sp.dma(ds_y[e_], ymix_d[t * 128:(t + 1) * 128, h * 256:(h + 1) * 256], yst[:, e_, :],
                           reads=[B_yst[e_]], parts=[B_ymix])

            load_head(0)
            pend = []

            def drain_one():
                (h2, qb2, v2, p2, f2, l2) = pend.pop(0)
                if f2:
                    load_szt(h2, qb2)
                if qb2 == 0 and f2 and h2 + 2 < 8:
                    ds_agl = K.dsem("ag_kv%d" % (h2 + 2), step=1)
                    pool.all_gather(ds_agl, kvg[h2 + 2], kvl[h2 + 2], reads=[B_kvl[h2 + 2]], writes=[B_kvg[h2 + 2]], qos="P2")
                if qb2 == 0 and f2 and h2 + 1 < 8:
                    load_head(h2 + 1)
                emit_pv(h2, qb2, v2, p2, f2)
                if l2:
                    emit_epilogue(h2, qb2)
            for idx, (h, qb, v_, first, last) in enumerate(allv):
                ps_ = emit_scores(h, qb, v_)
                pend.append((h, qb, v_, ps_, first, last))
                if len(pend) > LOOK:
                    drain_one()
            while pend:
                drain_one()
            K.barrier()
        if dbg and stop == 4:
            dbg_dump("ymix", ymix_d, [T, 4096], BF16, [B_ymix])

    if stop >= 5:
        with nc.sbuf_tensor("ymT", [128, 32, T], BF16) as ymT, nc.sbuf_tensor("yt", [128, 2, 4096], BF16) as yt, \
             nc.sbuf_tensor("wo0", [128, 32, 512], BF16) as wo0, nc.sbuf_tensor("wo1", [128, 32, 512], BF16) as wo1, \
             nc.sbuf_tensor("xr", [128, 4, 512], F32) as xr:
            WO = [wo0, wo1]
            B_wo = [Buf(), Buf()]
            ds_wo = [K.dsem("wo0"), K.dsem("wo1")]
            B_ymT = [Buf() for _ in range(NT)]
            B_yt = [Buf(), Buf()]
            ds_yt = [K.dsem("yt0"), K.dsem("yt1")]
            B_xr = [Buf() for _ in range(4)]
            ds_xr = [K.dsem("xr%d" % i) for i in range(4)]
            ds_xo = [K.dsem("xo%d" % i) for i in range(4)]

            def wo_load(cg, slot):
                pool.dma(ds_wo[slot], WO[slot][:, :, :],
                         w_out.rearrange("(kc p) n -> p kc n", p=128)[:, :, cg * 512:(cg + 1) * 512], writes=[B_wo[slot]])
            wo_load(0, 0)
            wo_load(1, 1)
            for t in range(NT):
                s = t % 2
                sp.dma(ds_yt[s], yt[:, s, :], ymix_d[t * 128:(t + 1) * 128, :], reads=[B_ymix], writes=[B_yt[s]])
                for q8 in range(8):
                    bank = (t * 8 + q8) % 8
                    pv = PS[bank][:].bitcast(BF16)
                    for j in range(4):
                        c = q8 * 4 + j
                        pe.op(lambda e: e.transpose(out=pv[:, j * 128:(j + 1) * 128], in_=yt[:, s, c * 128:(c + 1) * 128],
                                                    identity=ident[:]), reads=[B_yt[s], B_const], writes=[PB[bank]])
                    evac(ymT[:, q8 * 4:q8 * 4 + 4, t * 128:(t + 1) * 128], pv[:, 0:512].rearrange("p (a b) -> p a b", a=4),
                         reads=[PB[bank]], parts=[B_ymT[t]])
            xi = [0]
            for cg in range(4):
                slot = cg % 2
                for t in range(NT):
                    bank = t % 8
                    k_ = xi[0] % 4
                    xi[0] += 1
                    sp.dma(ds_xr[k_], xr[:, k_, :], x_d[t * 128:(t + 1) * 128, cg * 512:(cg + 1) * 512], writes=[B_xr[k_]])
                    for c in range(32):
                        pe.op(lambda e: e.matmul(out=PS[bank][:, :], lhsT=ymT[:, c, t * 128:(t + 1) * 128],
                                                 rhs=WO[slot][:, c, :], start=(c == 0), stop=(c == 31)),
                              reads=[B_wo[slot], B_ymT[t]], writes=[PB[bank]])
                    dve.op(lambda e: e.tensor_tensor(out=xr[:, k_, :], in0=PS[bank][:, :], in1=xr[:, k_, :], op=ALU.add),
                           reads=[PB[bank], B_xr[k_]], writes=[B_xr[k_]])
                    sp.dma(ds_xo[k_], x1_d[t * 128:(t + 1) * 128, cg * 512:(cg + 1) * 512], xr[:, k_, :],
                           reads=[B_xr[k_]], parts=[B_x1])
                if cg + 2 < 4:
                    wo_load(cg + 2, slot)
            K.barrier()
        if dbg and stop == 5:
            dbg_dump("x1", x1_d, [T, D], F32, [B_x1])

    if stop >= 6:
        sz_ctx = nc.sbuf_tensor("sz", [128, NCH, T], BF16)
        sz = sz_ctx.__enter__()
        xp_ctx = nc.sbuf_tensor("xpre", [128, NCH, T + 4], BF16)
        xpre = xp_ctx.__enter__()
        B_sz = [Buf() for _ in range(NCH)]
        B_xp = [Buf() for _ in range(NCH)]
        hst_ctx = nc.sbuf_tensor("hst", [128, NCH, 3], BF16)
        hst = hst_ctx.__enter__()
        hal_ctx = nc.sbuf_tensor("hal", [128, 4, NCH * 3], BF16)
        hal = hal_ctx.__enter__()
        hac_ctx = nc.sbuf_tensor("hac", [128, 2, NCH * 3], F32)
        hac = hac_ctx.__enter__()
        xn_ctx = nc.sbuf_tensor("xnT1", [128, KC, T], BF16)
        xnT1 = xn_ctx.__enter__()
        B_xn1 = [Buf() for _ in range(NT)]
        norm_phase(x1_d, B_x1, 4, xnT1, B_xn1, "_b")

        def l1_job(kind, g):
            col0 = (CW if kind == "z" else 0) + g * 384

            def comp(slot):
                for j in range(3):
                    n = g * 3 + j
                    for half in range(2):
                        bank = (n * 2 + half) % 8
                        for c in range(KC):
                            pe.op(lambda e: e.matmul(out=PS[bank][:, :], lhsT=WB[slot][:, c, j * 128:(j + 1) * 128],
                                                     rhs=xnT1[:, c, half * 512:(half + 1) * 512],
                                                     start=(c == 0), stop=(c == KC - 1)),
                                  reads=[B_wb[slot]] + B_xn1[half * 4:half * 4 + 4], writes=[PB[bank]])
                        if kind == "z":
                            act.op(lambda e: e.activation(out=sz[:, n, half * 512:(half + 1) * 512], in_=PS[bank][:, :],
                                                          func=AF.Silu), reads=[PB[bank]], parts=[B_sz[n]])
                        else:
                            evac(xpre[:, n, 3 + half * 512:3 + (half + 1) * 512], PS[bank][:, :], reads=[PB[bank]],
                                 parts=[B_xp[n]])
            return (wload(cw_in, col0, 384), comp)
        run_stream([l1_job("x", g) for g in range(7)])
        B_h = Buf()
        B_hl = Buf()
        B_hg = Buf()
        ds_h = K.dsem("halo")
        ds_hag = K.dsem("halo_ag", step=1)
        dve.op(lambda e: e.tensor_copy(out=hst[:], in_=xpre[:, :, T:T + 3]), reads=B_xp, writes=[B_h])
        sp.dma(ds_h, halo_l, hst[:].rearrange("p a b -> p (a b)"), reads=[B_h], writes=[B_hl])
        pool.all_gather(ds_hag, halo_g, halo_l, reads=[B_hl], writes=[B_hg])
        run_stream([l1_job("z", g) for g in range(7)])
        B_h2 = Buf()
        sp.dma(ds_h, hal[:], halo_g.rearrange("(r p) f -> p r f", p=128), reads=[B_hg], writes=[B_h2])
        dve.op(lambda e: e.tensor_scalar(out=hac[:, 0, :], in0=hal[:, 0, :], scalar1=masks[:, 4:5], scalar2=None, op0=ALU.mult),
               reads=[B_h2, B_const], writes=[B_h])
        for r in range(1, 4):
            dve.op(lambda e: e.scalar_tensor_tensor(out=hac[:, r % 2, :], in0=hal[:, r, :], scalar=masks[:, 4 + r:5 + r],
                                                    in1=hac[:, (r - 1) % 2, :], op0=ALU.mult, op1=ALU.add),
                   reads=[B_h2, B_h, B_const], writes=[B_h])
        dve.op(lambda e: e.tensor_copy(out=xpre[:, :, 0:3], in_=hac[:, 1, :].rearrange("p (a b) -> p a b", b=3)),
               reads=[B_h], parts=B_xp)
        K.barrier()
        xn_ctx.__exit__(None, None, None)
        hac_ctx.__exit__(None, None, None)
        hal_ctx.__exit__(None, None, None)
        hst_ctx.__exit__(None, None, None)
        if dbg and stop == 6:
            o = nc.dram_tensor("dbg_xpre", [128, NCH * (T + 4)], BF16, kind="ExternalOutput").ap()
            o2 = nc.dram_tensor("dbg_sz", [128, NCH * T], BF16, kind="ExternalOutput").ap()
            dsd = K.dsem("dbgl1")
            b = Buf()
            sp.dma(dsd, o, xpre[:].rearrange("p a b -> p (a b)"), reads=B_xp, writes=[b])
            sp.dma(dsd, o2, sz[:].rearrange("p a b -> p (a b)"), reads=B_sz, writes=[b])
            dbg_out["l1"] = b

    if stop >= 7:
        xc_ctx = nc.sbuf_tensor("xcb", [128, NCH, T], BF16)
        xcb = xc_ctx.__enter__()
        B_xc = [Buf() for _ in range(NCH)]
        with nc.sbuf_tensor("cwt", [128, NCH, 4], F32) as cwt, nc.sbuf_tensor("chs", [128, 10, NCH], F32) as chs, \
             nc.sbuf_tensor("idf", [128, 128], F32) as idf, nc.sbuf_tensor("dg", [128, 2, 4, 128], BF16) as dg, \
             nc.sbuf_tensor("gw", [128, 2, 2, 4, 128], BF16) as gw, nc.sbuf_tensor("cst", [128, 2, T], BF16) as cst, \
             nc.sbuf_tensor("stt", [128, 2, NCH], F32) as stt, nc.sbuf_tensor("sta", [128, 4, 2 * NCH], F32) as sta, \
             nc.sbuf_tensor("hin", [128, 6, NCH], F32) as hin:
            B_cc = Buf()
            ds_cc = K.dsem("cc")
            sp.dma(ds_cc, cwt[:], conv_w, writes=[B_cc])
            sp.dma(ds_cc, chs[:, 0:4, :], chv, parts=[B_cc])
            sp.dma(ds_cc, idf[:], ident_d, parts=[B_cc])
            dve.op(lambda e: e.memset(cst[:, 0, :], 0.0), parts=[B_cc])
            dve.op(lambda e: e.memset(cst[:, 1, :], 0.5), parts=[B_cc])
            act.op(lambda e: e.activation(out=chs[:, 4, :], in_=chs[:, 3, :], func=AF.Exp, scale=-1.0), reads=[B_cc], writes=[B_cc])
            act.op(lambda e: e.activation(out=chs[:, 5, :], in_=chs[:, 4, :], func=AF.Ln, bias=1.0, scale=1.0), reads=[B_cc], writes=[B_cc])
            dve.op(lambda e: e.tensor_scalar(out=chs[:, 6, :], in0=chs[:, 5, :], scalar1=-4.0, scalar2=None, op0=ALU.mult),
                   reads=[B_cc], writes=[B_cc])
            dve.op(lambda e: e.tensor_scalar(out=chs[:, 7:9, :], in0=chs[:, 1:3, :], scalar1=0.5, scalar2=None, op0=ALU.mult),
                   reads=[B_cc], writes=[B_cc])
            B_dg = [Buf(), Buf()]
            for n in range(NCH):
                s = n % 2
                for tau in range(4):
                    dve.op(lambda e: e.tensor_scalar(out=dg[:, s, tau, :], in0=idf[:], scalar1=cwt[:, n, tau:tau + 1], scalar2=None,
                                                     op0=ALU.mult), reads=[B_cc], parts=[B_dg[s]] if tau else (), writes=() if tau else [B_dg[s]])
                for half in range(2):
                    bank = (n * 2 + half) % 8
                    for tau in range(4):
                        pe.op(lambda e: e.matmul(out=PS[bank][:, :], lhsT=dg[:, s, tau, :],
                                                 rhs=xpre[:, n, tau + half * 512:tau + half * 512 + 512],
                                                 start=(tau == 0), stop=(tau == 3)),
                              reads=[B_dg[s], B_xp[n]], writes=[PB[bank]])
                    if half == 0:
                        act.op(lambda e: e.activation(out=xcb[:, n, half * 512:(half + 1) * 512], in_=PS[bank][:, :], func=AF.Identity,
                                                      bias=chs[:, 0, n:n + 1], scale=1.0), reads=[PB[bank], B_cc], parts=[B_xc[n]])
                    else:
                        dve.op(lambda e: e.tensor_scalar(out=xcb[:, n, half * 512:(half + 1) * 512], in0=PS[bank][:, :],
                                                         scalar1=chs[:, 0, n:n + 1], scalar2=None, op0=ALU.add),
                               reads=[PB[bank], B_cc], parts=[B_xc[n]])
            B_gw = [Buf(), Buf()]
            ds_gw = [K.dsem("gw0"), K.dsem("gw1")]
            B_ft = [[Buf() for _ in range(4)] for _ in range(3)]
            B_stt = Buf()

            def gw_load(n):
                s = n % 2
                pool.dma(ds_gw[s], gw[:, s, 0, :, :], ga_d[n], writes=[B_gw[s]])
                pool.dma(ds_gw[s], gw[:, s, 1, :, :], gx_d[n], parts=[B_gw[s]])
            gw_load(0)
            gw_load(1)
            for n in range(NCH):
                s = n % 2
                h_lo = (128 * n) // 168
                h_hi = (128 * n + 127) // 168
                m_lo = (168 * h_lo) // 128
                m_hi = (168 * (h_hi + 1) - 1) // 128
                sF = n % 3
                Fv = wball[:, sF * KC * 512:(sF + 1) * KC * 512].bitcast(F32)
                F = [Fv[:, k_ * T:(k_ + 1) * T] for k_ in range(4)]
                BF_ = B_ft[sF]
                for gi in range(2):
                    for half in range(2):
                        bank = gi * 2 + half + 4 * s
                        for mi, m in enumerate(range(m_lo, m_hi + 1)):
                            pe.op(lambda e: e.matmul(out=PS[bank][:, :], lhsT=gw[:, s, gi, mi, :],
                                                     rhs=xcb[:, m, half * 512:(half + 1) * 512],
                                                     start=(mi == 0), stop=(m == m_hi)),
                                  reads=[B_gw[s], B_xc[m]], writes=[PB[bank]])
                        act.op(lambda e: e.activation(out=F[gi][:, half * 512:(half + 1) * 512], in_=PS[bank][:, :],
                                                      func=AF.Tanh, bias=chs[:, 7 + gi, n:n + 1], scale=0.5),
                               reads=[PB[bank], B_cc], writes=[BF_[gi]] if half == 0 else (), parts=() if half == 0 else [BF_[gi]])
                if n + 2 < NCH:
                    gw_load(n + 2)
                act.op(lambda e: e.activation(out=F[2], in_=F[0], func=AF.Exp, scale=chs[:, 6, n:n + 1], bias=chs[:, 6, n:n + 1]),
                       reads=[BF_[0], B_cc], writes=[BF_[2]])
                act.op(lambda e: e.activation(out=F[3], in_=F[2], func=AF.Square), reads=[BF_[2]], writes=[BF_[3]])
                act.op(lambda e: e.activation(out=F[3], in_=F[3], func=AF.Sqrt, scale=-0.25, bias=0.25),
                       reads=[BF_[3]], writes=[BF_[3]])
                dve.op(lambda e: e.scalar_tensor_tensor(out=F[1], in0=F[1], scalar=1.0, in1=F[3], op0=ALU.add, op1=ALU.mult),
                       reads=[BF_[3], BF_[1]], writes=[BF_[1]])
                dve.op(lambda e: e.tensor_tensor(out=F[1], in0=F[1], in1=xcb[:, n, :], op=ALU.mult),
                       reads=[BF_[1], B_xc[n]], writes=[BF_[1]])
                dve.op(lambda e: e.tensor_tensor_scan(out=F[3], data0=F[2], data1=F[1], initial=0.0,
                                                      op0=ALU.mult, op1=ALU.add), reads=[BF_[2], BF_[1]], writes=[BF_[3]])
                dve.op(lambda e: e.tensor_tensor_scan(out=F[0], data0=F[2], data1=cst[:, 0, :], initial=1.0,
                                                      op0=ALU.mult, op1=ALU.add), reads=[BF_[2], B_cc], writes=[BF_[0]])
                dve.op(lambda e: e.tensor_copy(out=stt[:, 0, n:n + 1], in_=F[3][:, T - 1:T]),
                       reads=[BF_[3]], parts=[B_stt])
                dve.op(lambda e: e.tensor_copy(out=stt[:, 1, n:n + 1], in_=F[0][:, T - 1:T]),
                       reads=[BF_[0]], parts=[B_stt])
                pool.op(lambda e: e.tensor_tensor(out=xpre[:, n, 0:T], in0=F[0], in1=sz[:, n, :], op=ALU.mult),
                        reads=[BF_[0], B_sz[n]], writes=[B_xp[n]])
                pool.op(lambda e: e.tensor_tensor(out=sz[:, n, :], in0=F[3], in1=sz[:, n, :], op=ALU.mult),
                        reads=[BF_[3], B_sz[n]], writes=[B_sz[n]])
            WC = [wball[:, i * NCH * 512:(i + 1) * NCH * 512].rearrange("p (a b) -> p a b", a=NCH) for i in range(2)]
            B_wc = [Buf(), Buf()]
            ds_wc = [K.dsem("wc0"), K.dsem("wc1")]
            all_ft = [b_ for row in B_ft for b_ in row]

            def wc_load(cg, slot):
                pool.dma(ds_wc[slot], WC[slot][:, :, :],
                         cw_out.rearrange("(kc p) n -> p kc n", p=128)[:, :, cg * 512:(cg + 1) * 512], writes=[B_wc[slot]] + (all_ft if cg < 2 else []))
            wc_load(0, 0)
            wc_load(1, 1)
            B_sl = Buf()
            B_sg = Buf()
            B_sta = Buf()
            ds_st = K.dsem("st")
            ds_sag = K.dsem("st_ag", step=1)
            sp.dma(ds_st, st_l, stt[:].rearrange("p a b -> p (a b)"), reads=[B_stt], writes=[B_sl])
            pool.all_gather(ds_sag, st_g, st_l, reads=[B_sl], writes=[B_sg])
            sp.dma(ds_st, sta[:], st_g.rearrange("(r p) f -> p r f", p=128), reads=[B_sg], writes=[B_sta])
            dve.op(lambda e: e.memset(hin[:], 0.0), writes=[B_sta])
            for r in range(3):
                dve.op(lambda e: e.tensor_tensor(out=hin[:, 4, :], in0=sta[:, r, NCH:2 * NCH], in1=hin[:, r, :], op=ALU.mult),
                       reads=[B_sta], writes=[B_sta])
                dve.op(lambda e: e.tensor_tensor(out=hin[:, r + 1, :], in0=hin[:, 4, :], in1=sta[:, r, 0:NCH], op=ALU.add),
                       reads=[B_sta], writes=[B_sta])
            dve.op(lambda e: e.tensor_scalar(out=hin[:, 5, :], in0=hin[:, 0, :], scalar1=masks[:, 8:9], scalar2=None, op0=ALU.mult),
                   reads=[B_sta, B_const], writes=[B_sta])
            for r in range(1, 4):
                dve.op(lambda e: e.scalar_tensor_tensor(out=hin[:, 5, :], in0=hin[:, r, :], scalar=masks[:, 8 + r:9 + r],
                                                        in1=hin[:, 5, :], op0=ALU.mult, op1=ALU.add),
                       reads=[B_sta, B_const], writes=[B_sta])
            for n in range(NCH):
                dve.op(lambda e: e.scalar_tensor_tensor(out=sz[:, n, :], in0=xpre[:, n, 0:T], scalar=hin[:, 5, n:n + 1],
                                                        in1=sz[:, n, :], op0=ALU.mult, op1=ALU.add),
                       reads=[B_xp[n], B_sta, B_sz[n]], writes=[B_sz[n]])
            K.barrier()
        xc_ctx.__exit__(None, None, None)
        if dbg and stop == 7:
            o2 = nc.dram_tensor("dbg_hz", [128, NCH * T], BF16, kind="ExternalOutput").ap()
            dsd = K.dsem("dbgl2")
            b = Buf()
            sp.dma(dsd, o2, sz[:].rearrange("p a b -> p (a b)"), reads=B_sz, writes=[b])
            dbg_out["l2"] = b

    B_y = Buf()
    if stop >= 6:
        xp_ctx.__exit__(None, None, None)
    if stop >= 8:
        with nc.sbuf_tensor("x2", [128, NT, D], F32) as x2, nc.sbuf_tensor("gfn", [128, D], F32) as gfn, \
             nc.sbuf_tensor("fjk", [128, D], BF16) as fjk, nc.sbuf_tensor("fst", [128, NT, 4], F32) as fst:
            B_x2 = [Buf() for _ in range(NT)]
            ds_x2 = K.dsem("x2")
            ds_g2 = K.dsem("g2")
            ds_o = [K.dsem("o0"), K.dsem("o1")]
            B_gf = Buf()
            B_fj = Buf()
            B_fs = [Buf() for _ in range(NT)]
            sp.dma(ds_g2, gfn[:], vecs[5, :].partition_broadcast(128), writes=[B_gf])

            for t in range(NT):
                sp.dma(ds_x2, x2[:, t, :], x1_d[t * 128:(t + 1) * 128, :], reads=[B_x1], writes=[B_x2[t]])
            for cg in range(4):
                slot = cg % 2
                for t in range(NT):
                    bank = t % 8
                    for n in range(NCH):
                        pe.op(lambda e: e.matmul(out=PS[bank][:, :], lhsT=sz[:, n, t * 128:(t + 1) * 128],
                                                 rhs=WC[slot][:, n, :], start=(n == 0), stop=(n == NCH - 1)),
                              reads=[B_wc[slot], B_sz[n]], writes=[PB[bank]])
                    dve.op(lambda e: e.tensor_tensor(out=x2[:, t, cg * 512:(cg + 1) * 512], in0=PS[bank][:, :],
                                                     in1=x2[:, t, cg * 512:(cg + 1) * 512], op=ALU.add),
                           reads=[PB[bank], B_x2[t]], writes=[B_x2[t]])
                    if cg == 3:
                        act.op(lambda e: e.activation(out=fjk[:], in_=x2[:, t, :], func=AF.Square, accum_out=fst[:, t, 0:1]),
                               reads=[B_x2[t]], writes=[B_fj, B_fs[t]])
                        act.op(lambda e: e.activation(out=fst[:, t, 1:2], in_=fst[:, t, 0:1], func=AF.Sqrt, scale=1.0 / D, bias=EPS),
                               reads=[B_fs[t]], writes=[B_fs[t]])
                        dve.op(lambda e: e.reciprocal(out=fst[:, t, 2:3], in_=fst[:, t, 1:2]), reads=[B_fs[t]], writes=[B_fs[t]])
                        dve.op(lambda e: e.scalar_tensor_tensor(out=x2[:, t, :], in0=x2[:, t, :], scalar=fst[:, t, 2:3], in1=gfn[:],
                                                                op0=ALU.mult, op1=ALU.mult),
                               reads=[B_x2[t], B_fs[t], B_gf], writes=[B_x2[t]])
                        sp.dma(ds_o[t % 2], y_d[t * 128:(t + 1) * 128, :], x2[:, t, :], reads=[B_x2[t]], parts=[B_y])

                if cg + 2 < 4:
                    wc_load(cg + 2, slot)
    if stop >= 6:
        sz_ctx.__exit__(None, None, None)
    if stop < 8:
        with nc.sbuf_tensor("zz", [128, D], F32) as zz:
            bz = Buf()
            dve.op(lambda e: e.memset(zz[:], 0.0), writes=[bz])
            dsz = K.dsem("zz")
            for t in range(NT):
                sp.dma(dsz, y_d[t * 128:(t + 1) * 128, :], zz[:], reads=[bz], parts=[B_y])
            K.barrier()
    K.barrier()
    return nc


def _band(w):
    out = np.zeros((NCH, 128, 4, 128), np.float32)
    for n in range(NCH):
        h_lo = (128 * n) // 168
        h_hi = (128 * n + 127) // 168
        m_lo = (168 * h_lo) // 128
        for h in range(h_lo, h_hi + 1):
            j0 = max(128 * n, 168 * h)
            j1 = min(128 * n + 128, 168 * h + 168)
            for mi in range(4):
                m = m_lo + mi
                i0 = max(128 * m, 168 * h)
                i1 = min(128 * m + 128, 168 * h + 168)
                if i1 <= i0 or j1 <= j0:
                    continue
                out[n, i0 - 128 * m:i1 - 128 * m, mi, j0 - 128 * n:j1 - 128 * n] = \
                    w[h, i0 - 168 * h:i1 - 168 * h, j0 - 168 * h:j1 - 168 * h]
    return out


def make_in_maps(x, ab_norm, ab_w_in, ab_lambda, ab_head_norm, ab_sgu_ln_g, ab_sgu_ln_b, ab_sgu_w, ab_sgu_b,
                 ab_w_out, c_norm, c_w_in, c_conv_w, c_conv_b, c_gate_a_w, c_gate_a_b, c_gate_x_w, c_gate_x_b,
                 c_lambda, c_w_out, final_norm):
    f = lambda a: np.ascontiguousarray(np.asarray(a, dtype=np.float32))
    x = f(x)
    vecs = f(np.stack([f(ab_norm)[0], f(ab_head_norm)[0], f(ab_sgu_ln_g)[0], f(ab_sgu_ln_b)[0], f(c_norm)[0],
                       f(final_norm)]))
    chl = lambda v: f(f(v).reshape(NCH, 128).T)
    shared = {
        "w_in": f(ab_w_in)[0], "w_out": f(ab_w_out)[0], "cw_in": f(c_w_in)[0], "cw_out": f(c_w_out)[0],
        "vecs": vecs, "lam_p": f(ab_lambda)[0].reshape(512),
        "sgu_wT": f(np.transpose(f(ab_sgu_w)[0], (2, 0, 1))),
        "sgu_b": f(f(ab_sgu_b)[0].T),
        "conv_w": f(np.transpose(f(c_conv_w)[0].reshape(4, NCH, 128), (2, 1, 0))),
        "chv": f(np.stack([chl(f(c_conv_b)[0]), chl(f(c_gate_a_b)[0]), chl(f(c_gate_x_b)[0]), chl(f(c_lambda)[0])], axis=1)),
        "ga": _band(f(c_gate_a_w)[0]), "gx": _band(f(c_gate_x_w)[0]),
        "ident": np.eye(128, dtype=np.float32),
    }
    in_maps = []
    for core in range(8):
        b, r = core // 4, core % 4
        m = np.zeros((128, 16), np.float32)
        for rr in range(3):
            m[:, rr] = 0.0 if rr < r else NEG
        if r >= 1:
            m[:, 4 + r - 1] = 1.0
        m[:, 8 + r] = 1.0
        d = dict(shared)
        d["x"] = f(x[b, r * T:(r + 1) * T, :])
        d["masks"] = m
        in_maps.append(d)
    return in_maps


_CACHE = {}


def kernel(**inputs):
    in_maps = make_in_maps(**inputs)
    if "nc" not in _CACHE:
        _CACHE["nc"] = build_program()
    res = run_bass_kernel_spmd(_CACHE["nc"], in_maps, core_ids=list(range(8)))
    out = np.zeros((2, 4096, D), np.float32)
    for core in range(8):
        b, r = core // 4, core % 4
        out[b, r * T:(r + 1) * T, :] = res.results[core]["y"]
    return out
```
